# Optimizing a Trainium2 kernel written in Bass

```python
import jax, jax.numpy as jnp
from jax import lax
import numpy as np

D_MODEL = 2048
BATCH = 4
SEQ = 4096
DEPTH = 4

N_MIXERS = 2
D_FF = 5632
SWA_HEAD_DIM = 64
SWA_HEADS = D_MODEL // SWA_HEAD_DIM
SWA_KV_HEADS = SWA_HEADS // 8
SWA_WINDOW = 128
SWA_IN = (SWA_HEADS + 2 * SWA_KV_HEADS) * SWA_HEAD_DIM
NSA_HEAD_DIM = 128
NSA_HEADS = D_MODEL // NSA_HEAD_DIM
NSA_KV_HEADS = 4
CMP_BLOCK = 32
CMP_STRIDE = 16
CMP_HIDDEN = 2 * NSA_HEAD_DIM
SLC_BLOCK = 64
SLC_TOP_N = 16
NSA_WINDOW = 512
NSA_IN = NSA_HEADS * NSA_HEAD_DIM + 6 * NSA_KV_HEADS * NSA_HEAD_DIM + 3 * NSA_HEADS
ATTN_BLOCK = 128
SEL_CHUNK = 32
ROPE_THETA = 500000.0
ROPE_FRACTION = 4
NORM_EPS = 1e-6
NEG_INF = -1e30
FORCE_SCORE = 1e9
TINY = 1e-30
MAX_START_POS = 1024

kernel_name = "hybrid_swa_sink_nsa_macaron"


def rms_norm(x, g):
    xf = x.astype(jnp.float32)
    y = xf * lax.rsqrt(jnp.mean(xf * xf, axis=-1, keepdims=True) + NORM_EPS)
    return (y * g).astype(x.dtype)


def partial_rope(x, pos):
    d = x.shape[-1]
    rd = d // ROPE_FRACTION
    half = rd // 2
    inv = 1.0 / (ROPE_THETA ** (jnp.arange(half, dtype=jnp.float32) * (2.0 / rd)))
    ang = pos.astype(jnp.float32)[..., None] * inv
    ang = ang.reshape(ang.shape[:2] + (1,) * (x.ndim - 3) + (half,))
    cos, sin = jnp.cos(ang), jnp.sin(ang)
    x1, x2, rest = x[..., :half], x[..., half:rd], x[..., rd:]
    rot = jnp.concatenate([x1 * cos - x2 * sin, x2 * cos + x1 * sin], axis=-1).astype(x.dtype)
    return jnp.concatenate([rot, rest], axis=-1)


def swiglu(x, w_in, w_out):
    gate, up = jnp.split(x @ w_in, 2, axis=-1)
    return (jax.nn.silu(gate) * up) @ w_out


def banded_attention(q, k, v, window, sinks=None):
    B, T, G, R, d = q.shape
    nb = T // ATTN_BLOCK
    P = window // ATTN_BLOCK
    qb = q.reshape(B, nb, ATTN_BLOCK, G, R, d)
    pad = ((0, 0), (P, 0), (0, 0), (0, 0), (0, 0))
    kb = jnp.pad(k.reshape(B, nb, ATTN_BLOCK, G, d), pad)
    vb = jnp.pad(v.reshape(B, nb, ATTN_BLOCK, G, d), pad)
    kw = jnp.concatenate([kb[:, s:s + nb] for s in range(P + 1)], axis=2)
    vw = jnp.concatenate([vb[:, s:s + nb] for s in range(P + 1)], axis=2)
    s = jnp.einsum('bnqgrd,bnkgd->bgrnqk', qb, kw).astype(jnp.float32) * (d ** -0.5)
    qi = jnp.arange(ATTN_BLOCK)[:, None]
    ki = jnp.arange((P + 1) * ATTN_BLOCK)[None, :]
    rel = qi + P * ATTN_BLOCK - ki
    kpos = jnp.arange(nb)[:, None, None] * ATTN_BLOCK - P * ATTN_BLOCK + ki
    mask = (rel >= 0) & (rel < window) & (kpos >= 0)
    s = jnp.where(mask, s, NEG_INF)
    if sinks is None:
        p = jax.nn.softmax(s, axis=-1)
    else:
        sink_col = jnp.broadcast_to(sinks.astype(jnp.float32)[:, :, None, None, None], s.shape[:-1] + (1,))
        p = jax.nn.softmax(jnp.concatenate([s, sink_col], axis=-1), axis=-1)[..., :-1]
    o = jnp.einsum('bgrnqk,bnkgd->bnqgrd', p.astype(v.dtype), vw)
    return o.reshape(B, T, G, R, d)


def swa_mixer(x, positions, w_in, q_norm, k_norm, sinks, w_out):
    B, T, _ = x.shape
    G, d = SWA_KV_HEADS, SWA_HEAD_DIM
    R = SWA_HEADS // G
    q, k, v = jnp.split(x @ w_in, [SWA_HEADS * d, (SWA_HEADS + G) * d], axis=-1)
    q = partial_rope(rms_norm(q.reshape(B, T, G, R, d), q_norm), positions)
    k = partial_rope(rms_norm(k.reshape(B, T, G, d), k_norm), positions)
    o = banded_attention(q, k, v.reshape(B, T, G, d), SWA_WINDOW, sinks.reshape(G, R))
    return o.reshape(B, T, SWA_HEADS * d) @ w_out


def compress(t, pe, w1, w2, idx):
    blocks = t[:, idx] + pe[:, None, :]
    hdn = jax.nn.silu(jnp.einsum('bnlgd,ldh->bngh', blocks, w1))
    return jnp.einsum('bngh,hd->bngd', hdn, w2)


def compressed_attention(q, k, v, positions, k_norm, pe_k, w1_k, w2_k, pe_v, w1_v, w2_v):
    B, T, G, R, d = q.shape
    n_cmp = (T - CMP_BLOCK) // CMP_STRIDE + 1
    starts = jnp.arange(n_cmp) * CMP_STRIDE
    idx = starts[:, None] + jnp.arange(CMP_BLOCK)[None, :]
    end_idx = starts + CMP_BLOCK - 1
    k_c = partial_rope(rms_norm(compress(k, pe_k, w1_k, w2_k, idx), k_norm), positions[:, end_idx])
    v_c = compress(v, pe_v, w1_v, w2_v, idx)
    s = jnp.einsum('btgrd,bngd->bgrtn', q, k_c).astype(jnp.float32) * (d ** -0.5)
    mask = end_idx[None, :] <= jnp.arange(T)[:, None]
    s = jnp.where(mask, s, NEG_INF)
    e = jnp.exp(s - jnp.max(s, axis=-1, keepdims=True)) * mask
    p = e / jnp.maximum(jnp.sum(e, axis=-1, keepdims=True), TINY)
    o = jnp.einsum('bgrtn,bngd->btgrd', p.astype(v.dtype), v_c)
    return o, p


def select_blocks(p_cmp, T):
    n_cmp = p_cmp.shape[-1]
    n_slc = T // SLC_BLOCK
    cs = jnp.arange(n_cmp)[:, None] * CMP_STRIDE
    ss = jnp.arange(n_slc)[None, :] * SLC_BLOCK
    overlap = ((cs < ss + SLC_BLOCK) & (cs + CMP_BLOCK > ss)).astype(jnp.float32)
    imp = jnp.einsum('bgrtn,nj->bgtj', p_cmp, overlap)
    t = jnp.arange(T)[:, None]
    j = jnp.arange(n_slc)[None, :]
    cur = t // SLC_BLOCK
    forced = (j == 0) | (j == cur) | (j == cur - 1)
    imp = jnp.where(forced, FORCE_SCORE, jnp.where(j * SLC_BLOCK <= t, imp, NEG_INF))
    _, idx = lax.top_k(imp, min(SLC_TOP_N, n_slc))
    return idx.transpose(0, 2, 1, 3)


def selected_attention(q, k, v, blk_idx):
    B, T, G, R, d = q.shape
    n_slc = T // SLC_BLOCK
    n_top = blk_idx.shape[-1]
    kb = k.reshape(B, n_slc, SLC_BLOCK, G, d).transpose(0, 3, 1, 2, 4)
    vb = v.reshape(B, n_slc, SLC_BLOCK, G, d).transpose(0, 3, 1, 2, 4)
    nc = T // SEL_CHUNK
    qc = q.reshape(B, nc, SEL_CHUNK, G, R, d).transpose(1, 0, 2, 3, 4, 5)
    ic = blk_idx.reshape(B, nc, SEL_CHUNK, G, n_top).transpose(1, 0, 2, 3, 4)
    tc = jnp.arange(T).reshape(nc, SEL_CHUNK)
    b_ix = jnp.arange(B)[:, None, None, None]
    g_ix = jnp.arange(G)[None, None, :, None]

    def chunk(args):
        q_, i_, t_ = args
        k_sel = kb[b_ix, g_ix, i_]
        v_sel = vb[b_ix, g_ix, i_]
        s = jnp.einsum('bcgrd,bcgnld->bcgrnl', q_, k_sel).astype(jnp.float32) * (d ** -0.5)
        kpos = i_[..., None] * SLC_BLOCK + jnp.arange(SLC_BLOCK)
        mask = (kpos <= t_[None, :, None, None, None])[:, :, :, None]
        s = jnp.where(mask, s, NEG_INF)
        p = jax.nn.softmax(s.reshape(s.shape[:4] + (n_top * SLC_BLOCK,)), axis=-1).reshape(s.shape)
        return jnp.einsum('bcgrnl,bcgnld->bcgrd', p.astype(v.dtype), v_sel)

    o = lax.map(chunk, (qc, ic, tc))
    return o.transpose(1, 0, 2, 3, 4, 5).reshape(B, T, G, R, d)


def nsa_mixer(x, positions, w_in, q_norm, k_norm, pe_k, w1_k, w2_k, pe_v, w1_v, w2_v, w_out):
    B, T, _ = x.shape
    H, G, d = NSA_HEADS, NSA_KV_HEADS, NSA_HEAD_DIM
    R = H // G
    splits = np.cumsum([H * d] + [G * d] * 6).tolist()
    q, kc, vc, ks, vs, kw, vw, gates = jnp.split(x @ w_in, splits, axis=-1)
    q = partial_rope(rms_norm(q.reshape(B, T, G, R, d), q_norm), positions)
    kc, vc, ks, vs, kw, vw = [t.reshape(B, T, G, d) for t in (kc, vc, ks, vs, kw, vw)]
    o_cmp, p_cmp = compressed_attention(q, kc, vc, positions, k_norm, pe_k, w1_k, w2_k, pe_v, w1_v, w2_v)
    ks = partial_rope(rms_norm(ks, k_norm), positions)
    o_slc = selected_attention(q, ks, vs, select_blocks(p_cmp, T))
    kw = partial_rope(rms_norm(kw, k_norm), positions)
    o_win = banded_attention(q, kw, vw, NSA_WINDOW)
    g = jax.nn.sigmoid(gates.reshape(B, T, 3, G, R))[..., None]
    o = g[:, :, 0] * o_cmp + g[:, :, 1] * o_slc + g[:, :, 2] * o_win
    return o.reshape(B, T, H * d) @ w_out


def setup_inputs(seed: int = 0) -> dict:
    key = jax.random.key(seed)
    k = jax.random.split(key, 24)
    n_swa = len(range(0, DEPTH, N_MIXERS))
    n_nsa = len(range(1, DEPTH, N_MIXERS))

    def dense(kk, shape, fan_in):
        return jax.random.normal(kk, shape, jnp.float32) * (fan_in ** -0.5)

    def gain(kk, shape):
        return 1.0 + 0.02 * jax.random.normal(kk, shape, jnp.float32)

    hd = NSA_HEAD_DIM
    x = jax.random.normal(k[0], (BATCH, SEQ, D_MODEL), jnp.float32)
    positions = (jax.random.randint(k[1], (BATCH, 1), 0, MAX_START_POS, dtype=jnp.int32)
                 + jnp.arange(SEQ, dtype=jnp.int32)[None, :])
    return {
        'x': x,
        'positions': positions,
        'ffn1_norm': gain(k[2], (DEPTH, D_MODEL)),
        'ffn1_w_in': dense(k[3], (DEPTH, D_MODEL, 2 * D_FF), D_MODEL),
        'ffn1_w_out': dense(k[4], (DEPTH, D_FF, D_MODEL), D_FF),
        'mix_norm': gain(k[5], (DEPTH, D_MODEL)),
        'ffn2_norm': gain(k[6], (DEPTH, D_MODEL)),
        'ffn2_w_in': dense(k[7], (DEPTH, D_MODEL, 2 * D_FF), D_MODEL),
        'ffn2_w_out': dense(k[8], (DEPTH, D_FF, D_MODEL), D_FF),
        'swa_w_in': dense(k[9], (n_swa, D_MODEL, SWA_IN), D_MODEL),
        'swa_q_norm': gain(k[10], (n_swa, SWA_HEAD_DIM)),
        'swa_k_norm': gain(k[11], (n_swa, SWA_HEAD_DIM)),
        'swa_sinks': jax.random.normal(k[12], (n_swa, SWA_HEADS), jnp.float32),
        'swa_w_out': dense(k[13], (n_swa, SWA_HEADS * SWA_HEAD_DIM, D_MODEL), SWA_HEADS * SWA_HEAD_DIM),
        'nsa_w_in': dense(k[14], (n_nsa, D_MODEL, NSA_IN), D_MODEL),
        'nsa_q_norm': gain(k[15], (n_nsa, hd)),
        'nsa_k_norm': gain(k[16], (n_nsa, hd)),
        'nsa_cmp_pe_k': 0.1 * jax.random.normal(k[17], (n_nsa, CMP_BLOCK, hd), jnp.float32),
        'nsa_cmp_w1_k': dense(k[18], (n_nsa, CMP_BLOCK, hd, CMP_HIDDEN), CMP_BLOCK * hd),
        'nsa_cmp_w2_k': dense(k[19], (n_nsa, CMP_HIDDEN, hd), CMP_HIDDEN),
        'nsa_cmp_pe_v': 0.1 * jax.random.normal(k[20], (n_nsa, CMP_BLOCK, hd), jnp.float32),
        'nsa_cmp_w1_v': dense(k[21], (n_nsa, CMP_BLOCK, hd, CMP_HIDDEN), CMP_BLOCK * hd),
        'nsa_cmp_w2_v': dense(k[22], (n_nsa, CMP_HIDDEN, hd), CMP_HIDDEN),
        'nsa_w_out': dense(k[23], (n_nsa, NSA_HEADS * hd, D_MODEL), NSA_HEADS * hd),
    }


def reference(x, positions, ffn1_norm, ffn1_w_in, ffn1_w_out, mix_norm, ffn2_norm, ffn2_w_in, ffn2_w_out,
              swa_w_in, swa_q_norm, swa_k_norm, swa_sinks, swa_w_out,
              nsa_w_in, nsa_q_norm, nsa_k_norm, nsa_cmp_pe_k, nsa_cmp_w1_k, nsa_cmp_w2_k,
              nsa_cmp_pe_v, nsa_cmp_w1_v, nsa_cmp_w2_v, nsa_w_out):
    h = x
    for i in range(DEPTH):
        h = h + 0.5 * swiglu(rms_norm(h, ffn1_norm[i]), ffn1_w_in[i], ffn1_w_out[i])
        hn = rms_norm(h, mix_norm[i])
        j = i // N_MIXERS
        if i % N_MIXERS == 0:
            h = h + swa_mixer(hn, positions, swa_w_in[j], swa_q_norm[j], swa_k_norm[j], swa_sinks[j], swa_w_out[j])
        else:
            h = h + nsa_mixer(hn, positions, nsa_w_in[j], nsa_q_norm[j], nsa_k_norm[j],
                              nsa_cmp_pe_k[j], nsa_cmp_w1_k[j], nsa_cmp_w2_k[j],
                              nsa_cmp_pe_v[j], nsa_cmp_w1_v[j], nsa_cmp_w2_v[j], nsa_w_out[j])
        h = h + 0.5 * swiglu(rms_norm(h, ffn2_norm[i]), ffn2_w_in[i], ffn2_w_out[i])
    return h
```

```python
import numpy as np
import concourse.bass as bass
import concourse.mybir as mybir
from concourse.bass_utils import run_bass_kernel_spmd

F32 = mybir.dt.float32
BF16 = mybir.dt.bfloat16
I32 = mybir.dt.int32
U8 = mybir.dt.uint8
AF = mybir.ActivationFunctionType
ALU = mybir.AluOpType
AX = mybir.AxisListType

D = 2048
DFF = 5632
NTOK = 2048
EPS = 1e-6

ENGS = ("pe", "act", "dve", "pool", "sp")


class Op:
    __slots__ = ("eng", "fn", "deps", "dma_lane", "idx", "sig", "sigval", "pos", "inc", "epoch", "semi")

    def __init__(self, eng, fn, deps, dma_lane):
        self.eng = eng
        self.fn = fn
        self.deps = deps
        self.dma_lane = dma_lane
        self.sig = False
        self.sigval = None
        self.inc = 16


class Prog:
    def __init__(self, nc):
        self.nc = nc
        self.ops = []
        self.q = {e: [] for e in ENGS}
        self.last_w = {}
        self.readers = {}
        self.lanes = {}
        self.bar = None
        self.epoch = 0
        self.bar_pending = {e: False for e in ENGS}
        self.dma_since_bar = []

    def barrier(self):
        drains = []
        for e in ENGS:
            o = self.op(e, lambda eng: eng.drain())
            if e == "sp":
                for d in self.dma_since_bar:
                    o.deps.append(d)
            drains.append(o)
        self.bar = drains
        self.epoch += 1
        self.bar_pending = {e: True for e in ENGS}
        self.dma_since_bar = []
        self.last_w = {}
        self.readers = {}

    def op(self, eng, fn, reads=(), writes=(), lane=None):
        deps = {}
        if self.bar_pending[eng]:
            self.bar_pending[eng] = False
            for d in self.bar:
                deps[("bar", d.eng)] = d

        def add(d):
            if d is None:
                return
            if d.dma_lane is not None:
                deps[("d", d.idx)] = d
            else:
                k = ("e", d.eng)
                if k not in deps or deps[k].pos < d.pos:
                    deps[k] = d

        for r in reads:
            add(self.last_w.get(r))
        for r in writes:
            add(self.last_w.get(r))
            for rd in self.readers.get(r, {}).values():
                add(rd)
        o = Op(eng, fn, list(deps.values()), lane)
        o.epoch = self.epoch
        o.idx = len(self.ops)
        o.pos = len(self.q[eng])
        self.ops.append(o)
        self.q[eng].append(o)
        if lane is not None:
            self.dma_since_bar.append(o)
        for r in writes:
            self.last_w[r] = o
            self.readers[r] = {}
        for r in reads:
            k = ("d", o.idx) if lane is not None else ("e", eng)
            self.readers.setdefault(r, {})[k] = o
        return o

    def coll(self, fn, reads=(), writes=(), lane=None):
        o = self.op("pool", fn, reads, writes, lane)
        o.inc = 1
        return o

    def dma(self, eng, out, in_, reads=(), writes=(), lane=None):
        assert lane is not None
        return self.op(eng, ("dma", out, in_), reads, writes, lane)

    def emit(self, final_waits=()):
        nc = self.nc
        for o in self.ops:
            for d in o.deps:
                if d.dma_lane is not None or d.eng != o.eng or o.eng != "pe":
                    d.sig = True
        for o in final_waits:
            o.sig = True
        for o in self.ops:
            if o.dma_lane is not None:
                o.sig = True
        cnt = {e: 0 for e in ENGS}
        for e in ENGS:
            for o in self.q[e]:
                if o.sig and o.dma_lane is None:
                    cnt[e] += 1
                    o.sigval = cnt[e]
        lane_map = {}
        per_epoch = {}
        pool_cnt = {}
        dma_ops = sorted((o for o in self.ops if o.dma_lane is not None), key=lambda o: (o.epoch, ENGS.index(o.eng), o.pos))
        for o in dma_ops:
            key = (o.epoch, o.dma_lane)
            if key not in lane_map:
                lane_map[key] = per_epoch.get(o.epoch, 0)
                per_epoch[o.epoch] = lane_map[key] + 1
            o.semi = lane_map[key]
            pool_cnt[o.semi] = pool_cnt.get(o.semi, 0) + o.inc
            o.sigval = pool_cnt[o.semi]
        npool = max(per_epoch.values()) if per_epoch else 0
        import contextlib
        with contextlib.ExitStack() as st:
            esem = {e: st.enter_context(nc.semaphore("s_" + e)) for e in ENGS}
            lsem = {i: st.enter_context(nc.semaphore("l_%d" % i)) for i in range(npool)}
            block = st.enter_context(nc.Block())
            engobj = {}

            def run(e, eng):
                waited = {}
                for o in self.q[e]:
                    need = {}
                    for d in o.deps:
                        if d.dma_lane is not None:
                            key = ("l", d.semi)
                        elif d.eng != e or e != "pe":
                            key = ("e", d.eng)
                        else:
                            continue
                        if d.sigval > need.get(key, 0):
                            need[key] = d.sigval
                    for key, v in need.items():
                        if waited.get(key, 0) >= v:
                            continue
                        waited[key] = v
                        sem = lsem[key[1]] if key[0] == "l" else esem[key[1]]
                        eng.wait_ge(sem, v)
                    if isinstance(o.fn, tuple):
                        ins = eng.dma_start(out=o.fn[1], in_=o.fn[2])
                    else:
                        ins = o.fn(eng)
                    if o.sig:
                        if o.dma_lane is not None:
                            ins.then_inc(lsem[o.semi], o.inc)
                        else:
                            ins.then_inc(esem[e], 1)
                if e == "sp":
                    for o in final_waits:
                        sem = lsem[o.semi] if o.dma_lane is not None else esem[o.eng]
                        eng.wait_ge(sem, o.sigval)

            @block.tensor
            def _(eng):
                run("pe", eng)

            @block.scalar
            def _(eng):
                run("act", eng)

            @block.vector
            def _(eng):
                run("dve", eng)

            @block.gpsimd
            def _(eng):
                run("pool", eng)

            @block.sync
            def _(eng):
                run("sp", eng)


class Arena:
    def __init__(self, t, nbytes):
        self.t = t
        self.n = nbytes
        self.off = 0

    def alloc(self, shape, dt, at=None):
        esz = {F32: 4, BF16: 2, I32: 4, U8: 1}[dt]
        n = esz
        for s in shape[1:]:
            n *= s
        if at is None:
            at = (self.off + 31) // 32 * 32
            self.off = at + n
        assert at + n <= self.n, (at, n, self.n)
        v = self.t[0:shape[0], at:at + n].bitcast(dt)
        if len(shape) == 3:
            v = v.rearrange("p (a b) -> p a b", b=shape[2])
        elif len(shape) == 4:
            v = v.rearrange("p (a b c) -> p a b c", b=shape[2], c=shape[3])
        return v, at


NBUF_RES = [("nh", 0), ("nh", 1), ("nh", 2), ("xn", 0), ("xn", 1), ("xn", 2)]
ALIAS_ACT = [("actT", j, n) for j in range(20) for n in range(2)]


def load_gain(P, gain_bc, g_dram):
    P.dma("sp", gain_bc, g_dram.partition_broadcast(128), writes=[("gain",)], lane=("gain",))


def ring_load(P, ring, ring_state, src_list):
    s = ring_state["n"] % len(ring)
    ring_state["n"] += 1
    slot = ring[s]
    off = 0
    views = []
    for i, src in enumerate(src_list):
        shp = src.shape
        n = shp[1] * shp[2]
        dst = slot[:, off:off + n].rearrange("p (a b) -> p a b", b=shp[2])
        if len(src_list) == 1:
            P.dma("pool", dst, src, writes=[("ring", s, 0), ("ring", s, 1)], lane=("ring", s, 0))
        else:
            P.dma("pool", dst, src, reads=[("ringw", s, 1 - i)], writes=[("ring", s, i)], lane=("ring", s, i))
        views.append(dst)
        off += n
    return s, views


def norm_pass(P, A, ps, src, t0, res_fn, ntiles=8, src_fn=None, tiles=None):
    xnT, nbuf, ident = A["xnT"], A["nbuf"], A["ident"]
    gain_bc = nbuf["gain"]
    P.op("dve", lambda e: e.memset(nbuf["ss"][:, 0, 0:1], 0.0), writes=ALIAS_ACT + [("nbuf_fence",)])

    def stage_a(tt):
        r0 = t0 + tt * 128
        q = tt % 3
        hb = nbuf["h"][q]
        sap = src[r0:r0 + 128, :] if src_fn is None else src_fn(r0)
        P.dma("sp", hb, sap, reads=res_fn(r0) + [("nbuf_fence",)], writes=[("nh", q)], lane=("nh", q))
        ss = nbuf["ss"][:, q, 0:1]
        rs = nbuf["ss"][:, q, 1:2]
        xn = nbuf["xn"][q]
        P.op("act", lambda e, hb=hb, ss=ss, xn=xn: e.activation(out=xn, in_=hb, func=AF.Square, accum_out=ss),
             reads=[("nh", q), ("nbuf_fence",)], writes=[("xn", q), ("ss", q)])
        P.op("act", lambda e, ss=ss, rs=rs: e.activation(out=rs, in_=ss, func=AF.Sqrt, bias=A["eps"], scale=1.0 / D),
             reads=[("ss", q), ("eps",)], writes=[("rs", q)])
        P.op("dve", lambda e, rs=rs: e.reciprocal(out=rs, in_=rs), reads=[("rs", q)], writes=[("rs", q)])
        P.op("dve", lambda e, xn=xn, hb=hb, rs=rs: e.scalar_tensor_tensor(out=xn, in0=hb, scalar=rs, in1=gain_bc, op0=ALU.mult, op1=ALU.mult),
             reads=[("nh", q), ("rs", q), ("gain",)], writes=[("xn", q)])

    def stage_b(tt):
        q = tt % 3
        xn = nbuf["xn"][q]
        for half in range(2):
            pt = ps["tp"][half]
            for k in range(8):
                kc = half * 8 + k
                P.op("pe", lambda e, pt=pt, k=k, kc=kc, xn=xn: e.transpose(out=pt[:, k, :], in_=xn[:, kc * 128:(kc + 1) * 128], identity=ident),
                     reads=[("xn", q), ("ident",)], writes=[("pstp", half)])
            dst = xnT[:, half * 8:(half + 1) * 8, tt * 128:(tt + 1) * 128]
            if half == 0:
                P.op("act", lambda e, dst=dst, pt=pt: e.copy(out=dst, in_=pt), reads=[("pstp", half)], writes=[("xnT", tt, half)])
            else:
                P.op("dve", lambda e, dst=dst, pt=pt: e.tensor_copy(out=dst, in_=pt), reads=[("pstp", half)], writes=[("xnT", tt, half)])

    tl = list(range(ntiles) if tiles is None else tiles)
    for idx, tt in enumerate(tl):
        stage_a(tt)
        if idx >= 1:
            stage_b(tl[idx - 1])
    stage_b(tl[-1])


def outproj_pass(P, A, ps, ring, ring_state, srcT, src_res, nk, w_v, h_dram, t0, scale):
    misc = A["misc"]
    banks = [(ps["o"][0], ("pso", 0)), (ps["o"][1], ("pso", 1)), (ps["g"][0], ("psg", 0)), (ps["g"][1], ("psg", 1))]
    pieces = [(k0, min(16, nk - k0)) for k0 in range(0, nk, 16)]
    cnt = 0
    for c in range(D // 512):
        pv = []
        for (h0, hn) in pieces:
            s, (wv,) = ring_load(P, ring, ring_state, [w_v[:, h0:h0 + hn, c * 512:(c + 1) * 512]])
            pv.append((s, wv, h0, hn))
        for grp in range(2):
            for pi, (s, wv, h0, hn) in enumerate(pv):
                for t4 in range(4):
                    tt = grp * 4 + t4
                    po, pres = banks[t4]
                    if pi == 0:
                        r0 = t0 + tt * 128
                        b = (cnt + t4) % 2
                        if t4 < 2:
                            P.dma("sp", misc["hs"][b], h_dram[r0:r0 + 128, c * 512:(c + 1) * 512], reads=[("hdram", r0, c)], writes=[("hs", b)], lane=("hs", b))
                    for hh in range(hn):
                        hc = h0 + hh
                        P.op("pe", lambda e, po=po, wv=wv, hh=hh, hc=hc, tt=tt: e.matmul(po, lhsT=srcT[:, hc, tt * 128:(tt + 1) * 128], rhs=wv[:, hh, :], start=(hc == 0), stop=(hc == nk - 1)),
                             reads=[("ring", s, 0), ("ring", s, 1)] + src_res(hc, tt), writes=[pres])
                    if pi == len(pv) - 1:
                        r0 = t0 + tt * 128
                        b = (cnt + t4) % 2
                        hs, ho = misc["hs"][b], misc["ho"][b]
                        P.op("dve", lambda e, ho=ho, po=po, hs=hs: e.scalar_tensor_tensor(out=ho, in0=po, scalar=scale, in1=hs, op0=ALU.mult, op1=ALU.add),
                             reads=[pres, ("hs", b)], writes=[("ho", b)])
                        if t4 + 2 < 4:
                            r2 = t0 + (tt + 2) * 128
                            P.dma("sp", misc["hs"][b], h_dram[r2:r2 + 128, c * 512:(c + 1) * 512], reads=[("hdram", r2, c)], writes=[("hs", b)], lane=("hs", b))
                        P.dma("sp", h_dram[r0:r0 + 128, c * 512:(c + 1) * 512], ho, reads=[("ho", b)], writes=[("hdram", r0, c)], lane=("ho", b))
            cnt += 4


def ffn_phase(P, A, ps, h_dram, w_in, w_out, ring, ring_state, mid_hook=None):
    TT = 1024
    xnT, actT, misc = A["xnT"], A["actT"], A["misc"]
    w_in_v = w_in.rearrange("(kc p) n -> p kc n", p=128)
    w_out_v = w_out.rearrange("(hc p) n -> p hc n", p=128)
    for tp in range(NTOK // TT):
        t0 = tp * TT
        norm_pass(P, A, ps, h_dram, t0, lambda r0: [("hdram", r0, c) for c in range(4)])
        for jp in range(DFF // 256):
            s, (wg, wu) = ring_load(P, ring, ring_state, [w_in_v[:, :, jp * 256:(jp + 1) * 256], w_in_v[:, :, DFF + jp * 256:DFF + (jp + 1) * 256]])
            for jj in range(2):
                j = jp * 2 + jj
                for n in range(TT // 512):
                    b = (j * 2 + n) % 2
                    pg, pu = ps["g"][b], ps["u"][b]
                    for kc in range(16):
                        P.op("pe", lambda e, pg=pg, wg=wg, kc=kc, jj=jj, n=n: e.matmul(pg, lhsT=wg[:, kc, jj * 128:(jj + 1) * 128], rhs=xnT[:, kc, n * 512:(n + 1) * 512], start=(kc == 0), stop=(kc == 15)),
                             reads=[("ring", s, 0)] + [("xnT", n * 4 + q, kc // 8) for q in range(4)], writes=[("psg", b)])
                    for kc in range(16):
                        P.op("pe", lambda e, pu=pu, wu=wu, kc=kc, jj=jj, n=n: e.matmul(pu, lhsT=wu[:, kc, jj * 128:(jj + 1) * 128], rhs=xnT[:, kc, n * 512:(n + 1) * 512], start=(kc == 0), stop=(kc == 15)),
                             reads=[("ring", s, 1)] + [("xnT", n * 4 + q, kc // 8) for q in range(4)], writes=[("psu", b)])
                    sg = misc["sg"][b]
                    P.op("act", lambda e, sg=sg, pg=pg: e.activation(out=sg, in_=pg, func=AF.Silu), reads=[("psg", b)], writes=[("sg", b)])
                    dst = actT[:, j, n * 512:(n + 1) * 512]
                    P.op("dve", lambda e, dst=dst, sg=sg, pu=pu: e.tensor_tensor(out=dst, in0=sg, in1=pu, op=ALU.mult),
                         reads=[("sg", b), ("psu", b)], writes=[("actT", j, n)] + (NBUF_RES if j < 20 else []))
        if mid_hook is not None:
            mid_hook(tp)
        outproj_pass(P, A, ps, ring, ring_state, actT, lambda hc, tt: [("actT", hc, tt // 4)], 44, w_out_v, h_dram, t0, 0.5)


def init_consts(P, A, identd):
    P.dma("sp", A["ident"], identd, writes=[("ident",)], lane=("const",))
    P.op("dve", lambda e: e.memset(A["eps"], EPS), writes=[("eps",)])


def setup_common(nc, st):
    NB = 212000
    big = st.enter_context(nc.sbuf_tensor("arena", [128, NB], U8))
    ar = Arena(big, NB)
    A = {"big": big, "NB": NB}
    A["ident"], _ = ar.alloc([128, 128], BF16)
    A["eps"], _ = ar.alloc([128, 1], F32)
    misc = {}
    misc["sg"] = [ar.alloc([128, 512], F32)[0] for _ in range(2)]
    misc["hs"] = [ar.alloc([128, 512], F32)[0] for _ in range(2)]
    misc["ho"] = [ar.alloc([128, 512], F32)[0] for _ in range(2)]
    A["misc"] = misc
    ring = []
    for _ in range(4):
        v, at = ar.alloc([128, 8192], BF16)
        ring.append(v)
        if "ring0" not in A:
            A["ring0"] = at
    A["xnT"], A["z0"] = ar.alloc([128, 16, 1024], BF16)
    A["actT"], act_at = ar.alloc([128, 44, 1024], BF16)
    A["act_at"] = act_at
    assert act_at == A["z0"] + 32768
    sub = Arena(big, NB)
    sub.off = act_at
    nbuf = {}
    nbuf["h"] = [sub.alloc([128, 2048], F32)[0] for _ in range(3)]
    nbuf["xn"] = [sub.alloc([128, 2048], BF16)[0] for _ in range(3)]
    nbuf["gain"], _ = ar.alloc([128, 2048], F32)
    nbuf["ss"], _ = sub.alloc([128, 3, 2], F32)
    assert sub.off <= act_at + 20 * 2048
    A["nbuf"] = nbuf
    ps = {}
    ps["tp"] = [st.enter_context(nc.psum_tensor("ps_tp%d" % i, [128, 8, 128], BF16))[:] for i in range(2)]
    ps["g"] = [st.enter_context(nc.psum_tensor("ps_g%d" % i, [128, 512], F32))[:] for i in range(2)]
    ps["u"] = [st.enter_context(nc.psum_tensor("ps_u%d" % i, [128, 512], F32))[:] for i in range(2)]
    ps["o"] = [st.enter_context(nc.psum_tensor("ps_o%d" % i, [128, 512], F32))[:] for i in range(2)]
    return A, ps, ring, ar


CFG_SWA = dict(name="swa", H=32, R=8, dh=64, dv=64, rd=16, W=1, ncol=2560,
               kv_slabs=[("kv", 2048, 512)], nq=4)
CFG_NSA = dict(name="nsa", H=16, R=4, dh=128, dv=128, rd=32, W=4, ncol=5168,
               kv_slabs=[("kc", 2048, 512), ("vc", 2560, 512), ("ks", 3072, 512), ("vs", 3584, 512), ("kw", 4096, 512), ("vw", 4608, 512)], nq=4)
SCALE = {64: 64 ** -0.5, 128: 128 ** -0.5}


def head_post(P, M, pz, pz_res, nh, dh, gain_ap, cos_ap, sin_ap, rd, out_ap, q, out_res, defer=None):
    half = rd // 2
    sq = M["sq"][q][:, 0:nh * dh]
    y = M["y"][q][:, 0:nh * dh]
    ms = M["ms"][q][:, 0:nh]
    R = [("hp", q)]
    y3 = y.rearrange("p (h d) -> p h d", d=dh)
    P.op("act", lambda e: e.copy(out=y, in_=pz), reads=pz_res, writes=[("hp_y", q)])
    P.op("act", lambda e: e.activation(out=sq, in_=y, func=AF.Square), reads=[("hp_y", q)], writes=[("hp_sq", q)])
    P.op("dve", lambda e: e.tensor_reduce(out=ms, in_=sq.rearrange("p (h d) -> p h d", d=dh), axis=AX.X, op=ALU.add),
         reads=[("hp_sq", q)], writes=[("hp_ms", q)])
    def stage2():
        P.op("act", lambda e: e.activation(out=ms, in_=ms, func=AF.Sqrt, bias=M["eps"], scale=1.0 / dh), reads=[("hp_ms", q), ("eps",)], writes=[("hp_ms", q)])
        P.op("dve", lambda e: e.reciprocal(out=ms, in_=ms), reads=[("hp_ms", q)], writes=[("hp_ms", q)])
        P.op("dve", lambda e: e.tensor_tensor(out=y3, in0=y3, in1=ms.unsqueeze(2).to_broadcast([128, nh, dh]), op=ALU.mult),
             reads=[("hp_y", q), ("hp_ms", q)], writes=[("hp_y", q)])
        P.op("dve", lambda e: e.tensor_tensor(out=out_ap[:, :, rd:dh], in0=y3[:, :, rd:dh], in1=gain_ap[:, rd:dh].unsqueeze(1).to_broadcast([128, nh, dh - rd]), op=ALU.mult),
             reads=[("hp_y", q), ("qkgain",)], writes=out_res)
        P.op("dve", lambda e: e.tensor_tensor(out=y3[:, :, 0:rd], in0=y3[:, :, 0:rd], in1=gain_ap[:, 0:rd].unsqueeze(1).to_broadcast([128, nh, rd]), op=ALU.mult),
             reads=[("hp_y", q), ("qkgain",)], writes=[("hp_y", q)])
        t = M["rt"][q]
        cb = cos_ap.unsqueeze(1).to_broadcast([128, nh, half])
        sb = sin_ap.unsqueeze(1).to_broadcast([128, nh, half])
        y1, y2 = y3[:, :, 0:half], y3[:, :, half:rd]
        t1, t2, t3, t4 = [t[:, k, 0:nh * half].rearrange("p (h d) -> p h d", d=half) for k in range(4)]
        P.op("dve", lambda e: e.tensor_tensor(out=t1, in0=y1, in1=cb, op=ALU.mult), reads=[("hp_y", q), ("rope",)], writes=[("hp_t", q)])
        P.op("dve", lambda e: e.tensor_tensor(out=t2, in0=y2, in1=sb, op=ALU.mult), reads=[("hp_y", q), ("rope",)], writes=[("hp_t", q)])
        P.op("dve", lambda e: e.tensor_tensor(out=t3, in0=y2, in1=cb, op=ALU.mult), reads=[("hp_y", q), ("rope",)], writes=[("hp_t", q)])
        P.op("dve", lambda e: e.tensor_tensor(out=t4, in0=y1, in1=sb, op=ALU.mult), reads=[("hp_y", q), ("rope",)], writes=[("hp_t", q)])
        P.op("dve", lambda e: e.tensor_tensor(out=out_ap[:, :, 0:half], in0=t1, in1=t2, op=ALU.subtract), reads=[("hp_t", q)], writes=out_res)
        P.op("dve", lambda e: e.tensor_tensor(out=out_ap[:, :, half:rd], in0=t3, in1=t4, op=ALU.add), reads=[("hp_t", q)], writes=out_res)

    if defer is None:
        stage2()
    else:
        defer.append(stage2)


def rope_tables(P, M, pos_ap, ntile, half, invf_ap, cos_out, sin_out, tag):
    TWO_PI = 6.283185307179586
    posf = M["posf"][:, 0:ntile]
    ang = M["ang"][:, 0:ntile * half].rearrange("p (n d) -> p n d", d=half)
    a2 = M["ang2"][:, 0:ntile * half].rearrange("p (n d) -> p n d", d=half)
    P.op("dve", lambda e: e.tensor_copy(out=posf, in_=pos_ap), reads=[("pos", tag)], writes=[("posf",)])
    P.op("dve", lambda e: e.tensor_tensor(out=ang, in0=posf.unsqueeze(2).to_broadcast([128, ntile, half]),
                                          in1=invf_ap.unsqueeze(1).to_broadcast([128, ntile, half]), op=ALU.mult),
         reads=[("posf",), ("invf",)], writes=[("ang",)])
    ki = M["posi2"][:, 0:ntile * half].rearrange("p (n d) -> p n d", d=half)
    kf = M["ang3"][:, 0:ntile * half].rearrange("p (n d) -> p n d", d=half)

    def reduce_and_sin(src_shift, out_ap):
        P.op("dve", lambda e: e.tensor_scalar(out=a2, in0=ang, scalar1=src_shift, scalar2=None, op0=ALU.add), reads=[("ang",), ("rope",)], writes=[("ang2",)])
        P.op("dve", lambda e: e.tensor_scalar(out=kf, in0=a2, scalar1=1.0 / TWO_PI, scalar2=None, op0=ALU.mult), reads=[("ang2",)], writes=[("kf",)])
        P.op("dve", lambda e: e.tensor_copy(out=ki, in_=kf), reads=[("kf",)], writes=[("ki",)])
        P.op("dve", lambda e: e.tensor_copy(out=kf, in_=ki), reads=[("ki",)], writes=[("kf",)])
        P.op("dve", lambda e: e.scalar_tensor_tensor(out=a2, in0=kf, scalar=-TWO_PI, in1=a2, op0=ALU.mult, op1=ALU.add), reads=[("kf",), ("ang2",)], writes=[("ang2",)])
        P.op("dve", lambda e: e.tensor_scalar(out=kf, in0=a2, scalar1=float(np.pi), scalar2=-TWO_PI, op0=ALU.is_gt, op1=ALU.mult), reads=[("ang2",)], writes=[("kf",)])
        P.op("dve", lambda e: e.tensor_tensor(out=a2, in0=a2, in1=kf, op=ALU.add), reads=[("kf",), ("ang2",)], writes=[("ang2",)])
        P.op("dve", lambda e: e.tensor_scalar(out=a2, in0=a2, scalar1=float(np.pi), scalar2=-float(np.pi), op0=ALU.min, op1=ALU.max), reads=[("ang2",)], writes=[("ang2",)])
        P.op("act", lambda e: e.activation(out=out_ap, in_=a2, func=AF.Sin), reads=[("ang2",)], writes=[("rope",)])

    reduce_and_sin(0.0, sin_out)
    reduce_and_sin(float(np.pi / 2), cos_out)


def carve(A, base, spec):
    ar = Arena(A["big"], A["NB"])
    ar.off = base
    out = {}
    for item in spec:
        name, shape, dt = item[0], item[1], item[2]
        cnt = item[3] if len(item) > 3 else None
        if cnt is None:
            out[name] = ar.alloc(shape, dt)[0]
        else:
            out[name] = [ar.alloc(shape, dt)[0] for _ in range(cnt)]
    out["_end"] = ar.off
    return out


def mixer_p1(P, A, ps, ring, rs, cfg, io):
    nm = cfg["name"]
    dh, rd = cfg["dh"], cfg["rd"]
    half = rd // 2
    nhs = 512 // dh
    xnT = A["xnT"]
    M = carve(A, A["act_at"] + 40960, [
        ("sq", [128, 512], F32, 4), ("y", [128, 512], F32, 4), ("ms", [128, 8], F32, 4), ("rt", [128, 4, 64], F32, 4),
        ("outb", [128, 1024], BF16, 4), ("vout", [128, 512], BF16, 4), ("gout", [128, 48], F32, 4),
        ("posi", [128, 32], I32), ("posf", [128, 32], F32), ("ang", [128, 512], F32), ("ang2", [128, 512], F32), ("ang3", [128, 512], F32), ("posi2", [128, 512], I32),
        ("cos", [128, 32, 16], F32), ("sin", [128, 32, 16], F32), ("invf", [128, 16], F32),
        ("qg", [128, 128], F32), ("kg", [128, 128], F32)])
    assert M["_end"] <= A["act_at"] + 88 * 1024
    M["eps"] = A["eps"]
    P.dma("sp", M["posi"], io["pos_T"], writes=[("pos", "kv")], lane=("c1",))
    P.dma("sp", M["invf"][:, 0:half], io["invf"], writes=[("invf",)], lane=("c2",))
    P.dma("sp", M["qg"][:, 0:dh], io["qg"].partition_broadcast(128), writes=[("qkgain",)], lane=("c3",))
    P.dma("sp", M["kg"][:, 0:dh], io["kg"].partition_broadcast(128), writes=[("qkgain",)], lane=("c4",))
    cosv, sinv = M["cos"][:, :, 0:half], M["sin"][:, :, 0:half]
    rope_tables(P, M, M["posi"], 32, half, M["invf"][:, 0:half], cosv, sinv, "kv")
    for q in range(4):
        P.op("dve", lambda e, q=q: e.memset(M["outb"][q], 0.0), writes=[("outb", q)])
    load_gain(P, A["nbuf"]["gain"], io["mix_g"])
    w_in_v = io["w_in"].rearrange("(kc p) n -> p kc n", p=128)
    cnt = [0]
    pend = [[]]

    def run_pending(dl):
        prev, pend[0] = pend[0], dl
        for f in prev:
            f()

    def proj(w, s, ncols, tt):
        b = cnt[0] % 2
        cnt[0] += 1
        pz = ps["g"][b][:, 0:ncols]
        for kc in range(16):
            P.op("pe", lambda e, pz=pz, w=w, kc=kc, tt=tt: e.matmul(pz, lhsT=xnT[:, kc, tt * 128:(tt + 1) * 128], rhs=w[:, kc, :], start=(kc == 0), stop=(kc == 15)),
                 reads=[("ring", s, 0), ("ring", s, 1), ("xnT", tt, kc // 8)], writes=[("psg", b)])
        return pz, b

    W = cfg["W"]
    for tp in range(4):
        if nm == "swa":
            need = [tt for tt in range(8) if tp * 8 + tt >= 16 - W]
        else:
            need = list(range(8))
        if not need:
            continue
        norm_pass(P, A, ps, io["h_kv"], tp * 1024, lambda r0: [], src_fn=io.get("kv_src"), tiles=need)
        for (kind, c0, ncols) in cfg["kv_slabs"]:
            tts = [tt for tt in need if not (kind in ("kw", "vw") and tp * 8 + tt < 16 - W)]
            if not tts:
                continue
            s, (w,) = ring_load(P, ring, rs, [w_in_v[:, :, c0:c0 + ncols]])
            for tt in tts:
                ut = tp * 8 + tt
                pz, b = proj(w, s, ncols, tt)
                q = cnt[0] % 4
                if kind == "kv":
                    ob = M["outb"][q][:, 0:512].rearrange("p (h d) -> p h d", d=128)
                    dl = []
                    head_post(P, M, pz[:, 0:256], [("psg", b)], 4, 64, M["kg"], cosv[:, ut, :], sinv[:, ut, :], rd, ob, q, [("outb", q)], defer=dl)
                    dl.append(lambda ut=ut, q=q: P.dma("sp", io["Ks"][ut * 128:(ut + 1) * 128, :], M["outb"][q][:, 0:512], reads=[("outb", q)], writes=[("Ks", ut)], lane=("st", q)))
                    run_pending(dl)
                    vo = M["vout"][q][:, 0:256]
                    P.op("act", lambda e, vo=vo, pz=pz: e.copy(out=vo, in_=pz[:, 256:512]), reads=[("psg", b)], writes=[("vout", q)])
                    P.dma("sp", io["Vs"][ut * 128:(ut + 1) * 128, :], vo, reads=[("vout", q)], writes=[("Vs", ut)], lane=("sv", q))
                elif kind in ("ks", "kw"):
                    ob = M["outb"][q][:, 0:512].rearrange("p (h d) -> p h d", d=128)
                    dl = []
                    head_post(P, M, pz, [("psg", b)], 4, 128, M["kg"], cosv[:, ut, :], sinv[:, ut, :], rd, ob, q, [("outb", q)], defer=dl)
                    dl.append(lambda ut=ut, q=q, kind=kind: P.dma("sp", io[kind][ut * 128:(ut + 1) * 128, :], M["outb"][q][:, 0:512], reads=[("outb", q)], writes=[(kind, ut)], lane=("st", q)))
                    run_pending(dl)
                else:
                    vo = M["vout"][q]
                    P.op("act", lambda e, vo=vo, pz=pz: e.copy(out=vo, in_=pz), reads=[("psg", b)], writes=[("vout", q)])
                    P.dma("sp", io[kind][ut * 128:(ut + 1) * 128, :], vo, reads=[("vout", q)], writes=[(kind, ut)], lane=("sv", q))
        if tp >= 2:
            for qs in range(4):
                s, (w,) = ring_load(P, ring, rs, [w_in_v[:, :, qs * 512:(qs + 1) * 512]])
                for tt in range(8):
                    ut = tp * 8 + tt
                    ot = ut - 16
                    pz, b = proj(w, s, 512, tt)
                    q = cnt[0] % 4
                    ob = M["outb"][q][:, 0:nhs * 128].rearrange("p (h d) -> p h d", d=128)
                    dl = []
                    head_post(P, M, pz, [("psg", b)], nhs, dh, M["qg"], cosv[:, ut, :], sinv[:, ut, :], rd, ob, q, [("outb", q)], defer=dl)
                    dl.append(lambda ot=ot, qs=qs, q=q: P.dma("sp", io["Qs"][ot * 128:(ot + 1) * 128, qs * nhs * 128:(qs + 1) * nhs * 128], M["outb"][q][:, 0:nhs * 128],
                                                            reads=[("outb", q)], writes=[("Qs", ot, qs)], lane=("st", q)))
                    run_pending(dl)
            if nm == "nsa":
                s, (w,) = ring_load(P, ring, rs, [w_in_v[:, :, 5120:5168]])
                for tt in range(8):
                    ot = tp * 8 + tt - 16
                    pz, b = proj(w, s, 48, tt)
                    q = cnt[0] % 4
                    go = M["gout"][q]
                    P.op("act", lambda e, go=go, pz=pz: e.activation(out=go, in_=pz, func=AF.Sigmoid), reads=[("psg", b)], writes=[("gout", q)])
                    P.dma("sp", io["Gs"][ot * 128:(ot + 1) * 128, :], go, reads=[("gout", q)], writes=[("Gs", ot)], lane=("sg", q))
    run_pending([])


def mixer_p3(P, A, ps, ring, rs, cfg, io):
    nm = cfg["name"]
    H, R, dv, W = cfg["H"], cfg["R"], cfg["dv"], cfg["W"]
    nsa = nm == "nsa"
    NV = dv + 1
    NVC = 193
    spec = [("qtm", [128, H * 128], BF16, 2), ("qT", [128, H, 128], BF16, 2), ("ktm", [128, W + 1, 512], BF16, 2),
            ("kT", [128, W + 1, 4, 128], BF16, 2), ("vaug", [128, W + 1, 4, NV], BF16, 2), ("wmask", [128, W + 1, 512], BF16, 2),
            ("PT", [128, 16, 4, 128], BF16), ("acc", [128, 4, NVC], F32), ("den", [128, 8], F32), ("wgt", [128, 8], F32),
            ("obf", [128, 2048], BF16, 2), ("esink", [128, 32], F32), ("negtril", [128, 512], BF16)]
    if nsa:
        spec += [("oacc", [128, 2048], F32), ("otmp", [128, 512], F32), ("gates", [128, 48], F32, 2),
                 ("cmask", [128, 2, 512], BF16, 2), ("selA", [128, 64], F32, 2), ("selC", [128, 64], F32, 2), ("selV", [128, 64], F32, 2),
                 ("imp", [128, 64], F32), ("val", [128, 64], F32), ("val2", [128, 64], F32), ("m8", [128, 16], F32),
                 ("selb", [128, 64], BF16), ("nselb", [128, 64], BF16), ("nselT", [64, 4, 512], BF16), ("E", [64, 32, 128], BF16),
                 ("ksT", [128, 4, 4096], BF16), ("vsb", [128, 8, NV], BF16, 3), ("ktm2", [128, 512], BF16, 2)]
    M = carve(A, A["ring0"], spec)
    top = A["act_at"] + 80 * 1024
    assert M["_end"] <= top, (M["_end"], top)
    if nsa:
        M.update({k: v for k, v in carve(A, top, [("kcT", [128, 4, 256], BF16), ("vcaug", [128, 2, 4, NVC], BF16)]).items() if k != "_end"})
    M["ptcnt"] = [0]
    M["ucnt"] = [0]
    M["qpar"] = [0]
    return M


NEG = -30000.0


def attend(P, M, ps, cfg, tag, g, hq, chunks, kT_fn, v_fn, mask_fn, nv, pre_batch=None):
    qT = M["qTcur"]
    scale = SCALE[128] if cfg["dh"] == 128 else SCALE[64]
    acc = M["acc"]
    nb = (len(chunks) + 7) // 8
    for bi in range(nb):
        cb = chunks[bi * 8:(bi + 1) * 8]
        if pre_batch is not None:
            pre_batch(bi, cb)
        par = M["ptcnt"][0] % 2
        M["ptcnt"][0] += 1
        for k, ci in enumerate(cb):
            b = M["ucnt"][0] % 2
            M["ucnt"][0] += 1
            sT = ps["u"][b]
            kap, kres = kT_fn(ci)
            adds = mask_fn(ci)
            P.op("pe", lambda e, sT=sT, kap=kap, last=(len(adds) == 0): e.matmul(sT, lhsT=kap, rhs=qT[:, hq:hq + 4, :], start=True, stop=last),
                 reads=kres + [("qT", M["qpar"][0])], writes=[("psu", b)])
            for ai, (la, ra, rres) in enumerate(adds):
                P.op("pe", lambda e, sT=sT, la=la, ra=ra, last=(ai == len(adds) - 1): e.matmul(sT, lhsT=la, rhs=ra, start=False, stop=last),
                     reads=rres, writes=[("psu", b)])
            slot = par * 8 + k
            pt = M["PT"][:, slot]
            P.op("act", lambda e, pt=pt, sT=sT: e.activation(out=pt, in_=sT.rearrange("p (h t) -> p h t", t=128), func=AF.Exp, scale=scale),
                 reads=[("psu", b)], writes=[("PT", slot)])
        for hh in range(4):
            po = ps["o"][hh // 2][:, 0:2 * nv].rearrange("p (h e) -> p h e", e=nv)[:, hh % 2, :]
            for k, ci in enumerate(cb):
                vap, vres = v_fn(ci)
                slot = par * 8 + k
                P.op("pe", lambda e, po=po, slot=slot, hh=hh, vap=vap, k=k: e.matmul(po, lhsT=M["PT"][:, slot, hh, :], rhs=vap, start=(k == 0), stop=(k == len(cb) - 1)),
                     reads=[("PT", slot)] + vres, writes=[("pso", hh // 2)])
        for pb in range(2):
            src = ps["o"][pb][:, 0:2 * nv].rearrange("p (h e) -> p h e", e=nv)
            dst = acc[:, 2 * pb:2 * pb + 2, 0:nv]
            if bi == 0:
                P.op("dve", lambda e, dst=dst, src=src: e.tensor_copy(out=dst, in_=src), reads=[("pso", pb)], writes=[("acc",)])
            else:
                P.op("dve", lambda e, dst=dst, src=src: e.tensor_tensor(out=dst, in0=dst, in1=src, op=ALU.add), reads=[("pso", pb), ("acc",)], writes=[("acc",)])


def mixer_p3_run(P, A, ps, ring, rs, cfg, io, M):
    nm = cfg["name"]
    H, R, dv, W = cfg["H"], cfg["R"], cfg["dv"], cfg["W"]
    nsa = nm == "nsa"
    NV = dv + 1
    ident = A["ident"]
    Kw = io["kw"] if nsa else io["Ks"]
    Vw = io["vw"] if nsa else io["Vs"]
    P.dma("sp", M["negtril"], io["tril"], writes=[("negtril",)], lane=("c1",))
    for q in range(2):
        P.op("dve", lambda e, q=q: e.memset(M["vaug"][q], 1.0), writes=[("vaug_init", q)])
    if not nsa:
        P.dma("sp", M["esink"], io["sinks"].partition_broadcast(128), writes=[("esink",)], lane=("c2",))
        P.op("act", lambda e: e.activation(out=M["esink"], in_=M["esink"], func=AF.Exp), reads=[("esink",)], writes=[("esink",)])
    else:
        P.dma("sp", M["E"], io["E"], writes=[("E",)], lane=("c2",))
        for q in range(3):
            P.op("dve", lambda e, q=q: e.memset(M["vsb"][q], 1.0), writes=[("vsb_init", q)])
        for ut in range(32):
            q = ut % 2
            P.dma("sp", M["ktm2"][q], io["ks"][ut * 128:(ut + 1) * 128, :], writes=[("ktm2", q)], lane=("lk2", q))
            pt = ps["tp"][q]
            for g in range(4):
                P.op("pe", lambda e, pt=pt, g=g, q=q: e.transpose(out=pt[:, g, :], in_=M["ktm2"][q][:, g * 128:(g + 1) * 128], identity=ident),
                     reads=[("ktm2", q), ("ident",)], writes=[("pstp", q)])
            dst = M["ksT"][:, :, ut * 128:(ut + 1) * 128]
            if q == 0:
                P.op("act", lambda e, dst=dst, pt=pt: e.copy(out=dst, in_=pt[:, 0:4, :]), reads=[("pstp", q)], writes=[("ksT", ut)])
            else:
                P.op("dve", lambda e, dst=dst, pt=pt: e.tensor_copy(out=dst, in_=pt[:, 0:4, :]), reads=[("pstp", q)], writes=[("ksT", ut)])

    def tile_loads(i):
        par = i % 2
        ut = 16 + i
        c0 = ut - W
        P.dma("sp", M["qtm"][par], io["Qs"][i * 128:(i + 1) * 128, :], writes=[("qtm", par)], lane=("lq", par))
        P.dma("sp", M["wmask"][par], io["wmask"][i], writes=[("wmask", par)], lane=("lm", par))
        P.dma("sp", M["ktm"][par], Kw[c0 * 128:(ut + 1) * 128, :].rearrange("(w p) c -> p w c", p=128), writes=[("ktm", par)], lane=("lk", par))
        for w in range(W + 1):
            P.dma("sp", M["vaug"][par][:, w, :, 0:dv], Vw[(c0 + w) * 128:(c0 + w + 1) * 128, :].rearrange("p (g d) -> p g d", d=dv),
                  reads=[("vaug_init", par)], writes=[("vaug", par, w)], lane=("lv", par, w))
        if nsa:
            P.dma("sp", M["gates"][par], io["Gs"][i * 128:(i + 1) * 128, :], writes=[("gates", par)], lane=("lg", par))
            P.dma("sp", M["cmask"][par], io["cmask"][i], writes=[("cmask", par)], lane=("lm2", par))
            P.dma("sp", M["selA"][par], io["selA"][i], writes=[("selA", par)], lane=("ls1", par))
            P.dma("sp", M["selC"][par], io["selC"][i], writes=[("selC", par)], lane=("ls2", par))
            P.dma("sp", M["selV"][par], io["selV"][i], writes=[("selV", par)], lane=("ls3", par))

    vsb_cnt = [0]
    tile_loads(0)
    for i in range(16):
        ut = 16 + i
        par = i % 2
        for hb in range(H // 8):
            pt = ps["tp"][hb % 2]
            for k in range(8):
                hd = hb * 8 + k
                P.op("pe", lambda e, pt=pt, k=k, hd=hd, par=par: e.transpose(out=pt[:, k, :], in_=M["qtm"][par][:, hd * 128:(hd + 1) * 128], identity=ident),
                     reads=[("qtm", par), ("ident",)], writes=[("pstp", hb % 2)])
            dst = M["qT"][par][:, hb * 8:(hb + 1) * 8, :]
            if hb % 2 == 0:
                P.op("act", lambda e, dst=dst, pt=pt: e.copy(out=dst, in_=pt), reads=[("pstp", hb % 2)], writes=[("qT", par)])
            else:
                P.op("dve", lambda e, dst=dst, pt=pt: e.tensor_copy(out=dst, in_=pt), reads=[("pstp", hb % 2)], writes=[("qT", par)])
        for w in range(W + 1):
            pt = ps["tp"][w % 2]
            for g in range(4):
                P.op("pe", lambda e, pt=pt, g=g, w=w, par=par: e.transpose(out=pt[:, g, :], in_=M["ktm"][par][:, w, g * 128:(g + 1) * 128], identity=ident),
                     reads=[("ktm", par), ("ident",)], writes=[("pstp", w % 2)])
            dst = M["kT"][par][:, w]
            if w % 2 == 0:
                P.op("act", lambda e, dst=dst, pt=pt: e.copy(out=dst, in_=pt[:, 0:4, :]), reads=[("pstp", w % 2)], writes=[("kT", par)])
            else:
                P.op("dve", lambda e, dst=dst, pt=pt: e.tensor_copy(out=dst, in_=pt[:, 0:4, :]), reads=[("pstp", w % 2)], writes=[("kT", par)])
        if i + 1 < 16:
            tile_loads(i + 1)
        M["qTcur"] = M["qT"][par]
        M["qpar"][0] = par
        obf = M["obf"][par]
        for g in range(4):
            for qd in range(R // 4):
                hq = g * R + qd * 4
                den, wgt = M["den"][:, 0:4], M["wgt"][:, 0:4]
                branches = ["cmp", "win", "sel"] if nsa else ["win"]
                for br, bname in enumerate(branches):
                    if bname == "win":
                        attend(P, M, ps, cfg, bname, g, hq, list(range(W + 1)),
                               lambda ci: (M["kT"][par][:, ci, g, :], [("kT", par)]),
                               lambda ci: (M["vaug"][par][:, ci, g, :], [("vaug", par, ci)]),
                               lambda ci: [(ident, M["wmask"][par][:, ci, :], [("wmask", par), ("ident",)])], NV)
                    elif bname == "cmp":
                        attend(P, M, ps, cfg, bname, g, hq, [0, 1],
                               lambda ci: (M["kcT"][:, g, ci * 128:(ci + 1) * 128], [("kcT",)]),
                               lambda ci: (M["vcaug"][:, ci, g, :], [("vcaug",)]),
                               lambda ci: [(ident, M["cmask"][par][:, ci, :], [("cmask", par), ("ident",)])], 193)
                    else:
                        nch = ut + 1
                        vslot = {}

                        def load_vbatch(bi, g=g, vslot=vslot, nch=nch):
                            cb = list(range(nch))[bi * 8:(bi + 1) * 8]
                            if not cb or cb[0] in vslot:
                                return
                            q = vsb_cnt[0] % 3
                            vsb_cnt[0] += 1
                            for k, ck in enumerate(cb):
                                vslot[ck] = (q, k)
                                P.dma("sp", M["vsb"][q][:, k, 0:dv], io["vs"][ck * 128:(ck + 1) * 128, g * dv:(g + 1) * dv],
                                      reads=[("vsb_init", q)], writes=[("vsb", q, k)], lane=("lvs", q, k))

                        def pre_batch(bi, cb, load_vbatch=load_vbatch):
                            load_vbatch(bi)
                            load_vbatch(bi + 1)

                        def sel_mask(ci, g=g, ut=ut):
                            adds = [(M["E"][:, ci, :], M["nselT"][:, g, :], [("E",), ("nselT", g)])]
                            if ci == ut:
                                adds.append((ident, M["negtril"], [("negtril",), ("ident",)]))
                            return adds
                        attend(P, M, ps, cfg, bname, g, hq, list(range(nch)),
                               lambda ci: (M["ksT"][:, g, ci * 128:(ci + 1) * 128], [("ksT", ci)]),
                               lambda ci: (M["vsb"][vslot[ci][0]][:, vslot[ci][1], :], [("vsb", vslot[ci][0], vslot[ci][1])]),
                               sel_mask, NV, pre_batch=pre_batch)
                    acc = M["acc"]
                    dcol = acc[:, :, dv]
                    if not nsa:
                        P.op("dve", lambda e, dcol=dcol, hq=hq: e.tensor_tensor(out=den, in0=dcol, in1=M["esink"][:, hq:hq + 4], op=ALU.add),
                             reads=[("acc",), ("esink",)], writes=[("den",)])
                    else:
                        P.op("dve", lambda e, dcol=dcol: e.tensor_scalar(out=den, in0=dcol, scalar1=1e-30, scalar2=None, op0=ALU.max),
                             reads=[("acc",)], writes=[("den",)])
                    P.op("dve", lambda e: e.reciprocal(out=den, in_=den), reads=[("den",)], writes=[("den",)])
                    if not nsa:
                        dst = obf.rearrange("p (h d) -> p h d", d=dv)[:, hq:hq + 4, :]
                        P.op("dve", lambda e, dst=dst: e.tensor_tensor(out=dst, in0=acc[:, :, 0:dv], in1=den.unsqueeze(2).to_broadcast([128, 4, dv]), op=ALU.mult),
                             reads=[("acc",), ("den",)], writes=[("obf", par)])
                        continue
                    gi = {"cmp": 0, "sel": 1, "win": 2}[bname]
                    gsl = M["gates"][par][:, gi * 16 + hq:gi * 16 + hq + 4]
                    P.op("dve", lambda e, gsl=gsl: e.tensor_tensor(out=wgt, in0=den, in1=gsl, op=ALU.mult), reads=[("den",), ("gates", par)], writes=[("wgt",)])
                    oa = M["oacc"].rearrange("p (h d) -> p h d", d=dv)[:, hq:hq + 4, :]
                    if br == 0:
                        P.op("dve", lambda e, oa=oa: e.tensor_tensor(out=oa, in0=acc[:, :, 0:dv], in1=wgt.unsqueeze(2).to_broadcast([128, 4, dv]), op=ALU.mult),
                             reads=[("acc",), ("wgt",)], writes=[("oacc",)])
                        imp = M["imp"]
                        for hh in range(4):
                            if hh == 0:
                                P.op("dve", lambda e: e.tensor_scalar(out=imp, in0=acc[:, 0, 129:193], scalar1=den[:, 0:1], scalar2=None, op0=ALU.mult),
                                     reads=[("acc",), ("den",)], writes=[("imp",)])
                            else:
                                P.op("dve", lambda e, hh=hh: e.scalar_tensor_tensor(out=imp, in0=acc[:, hh, 129:193], scalar=den[:, hh:hh + 1], in1=imp, op0=ALU.mult, op1=ALU.add),
                                     reads=[("acc",), ("den",), ("imp",)], writes=[("imp",)])
                        val, val2, m8 = M["val"], M["val2"], M["m8"]
                        sA, sC, sV = M["selA"][par], M["selC"][par], M["selV"][par]
                        P.op("dve", lambda e, sA=sA: e.tensor_tensor(out=val, in0=imp, in1=sA, op=ALU.mult), reads=[("imp",), ("selA", par)], writes=[("val",)])
                        P.op("dve", lambda e, sC=sC: e.tensor_tensor(out=val, in0=val, in1=sC, op=ALU.add), reads=[("val",), ("selC", par)], writes=[("val",)])
                        P.op("dve", lambda e: e.max(out=m8[:, 0:8], in_=val), reads=[("val",)], writes=[("m8",)])
                        P.op("dve", lambda e: e.match_replace(out=val2, in_to_replace=m8[:, 0:8], in_values=val, imm_value=-3.0e38), reads=[("val",), ("m8",)], writes=[("val2",)])
                        P.op("dve", lambda e: e.max(out=m8[:, 8:16], in_=val2), reads=[("val2",)], writes=[("m8",)])
                        P.op("dve", lambda e: e.tensor_scalar(out=val2, in0=val, scalar1=m8[:, 15:16], scalar2=None, op0=ALU.is_ge), reads=[("val",), ("m8",)], writes=[("val2",)])
                        P.op("dve", lambda e, sV=sV: e.tensor_tensor(out=M["selb"], in0=val2, in1=sV, op=ALU.mult), reads=[("val2",), ("selV", par)], writes=[("selb",)])
                        if "dbg_sel" in io:
                            P.dma("sp", io["dbg_sel"][i, g], M["selb"], reads=[("selb",)], writes=[("dbgsel", i, g)], lane=("dbg1",))
                            P.dma("sp", io["dbg_imp"][i, g], M["imp"], reads=[("imp",)], writes=[("dbgimp", i, g)], lane=("dbg2",))
                            P.dma("sp", io["dbg_val"][i, g], M["val"], reads=[("val",)], writes=[("dbgval", i, g)], lane=("dbg3",))
                            P.dma("sp", io["dbg_m8"][i, g], M["m8"], reads=[("m8",)], writes=[("dbgm8", i, g)], lane=("dbg4",))
                        P.op("dve", lambda e: e.tensor_scalar(out=M["nselb"], in0=M["selb"], scalar1=-NEG, scalar2=NEG, op0=ALU.mult, op1=ALU.add),
                             reads=[("selb",)], writes=[("nselb",)])
                        pt = ps["tp"][0]
                        P.op("pe", lambda e, pt=pt: e.transpose(out=pt[0:64, 0, :], in_=M["nselb"], identity=ident), reads=[("nselb",), ("ident",)], writes=[("pstp", 0)])
                        P.op("dve", lambda e, pt=pt, g=g: e.tensor_copy(out=M["nselT"][:, g, :].rearrange("p (h t) -> p h t", t=128),
                                                                        in_=pt[0:64, 0, :].unsqueeze(1).to_broadcast([64, 4, 128])),
                             reads=[("pstp", 0)], writes=[("nselT", g)])
                    else:
                        ot = M["otmp"].rearrange("p (h d) -> p h d", d=dv)
                        P.op("dve", lambda e, ot=ot: e.tensor_tensor(out=ot, in0=acc[:, :, 0:dv], in1=wgt.unsqueeze(2).to_broadcast([128, 4, dv]), op=ALU.mult),
                             reads=[("acc",), ("wgt",)], writes=[("otmp",)])
                        P.op("dve", lambda e, oa=oa, ot=ot: e.tensor_tensor(out=oa, in0=oa, in1=ot, op=ALU.add), reads=[("otmp",), ("oacc",)], writes=[("oacc",)])
        if nsa:
            P.op("act", lambda e, obf=obf: e.copy(out=obf, in_=M["oacc"]), reads=[("oacc",)], writes=[("obf", par)])
        P.dma("sp", io["Os"][i * 128:(i + 1) * 128, :], obf, reads=[("obf", par)], writes=[("Os", i)], lane=("so", par))


def mixer_p4(P, A, ps, ring, rs, cfg, io):
    xnT, ident = A["xnT"], A["ident"]
    M = carve(A, A["act_at"], [("otm", [128, 2048], BF16, 2)])
    w_out_v = io["w_out"].rearrange("(kc p) n -> p kc n", p=128)
    for p in range(2):
        for tt in range(8):
            i = p * 8 + tt
            q = tt % 2
            P.dma("sp", M["otm"][q], io["Os"][i * 128:(i + 1) * 128, :], writes=[("otm", q)], lane=("lo", q))
            for half in range(2):
                pt = ps["tp"][half]
                for k in range(8):
                    kc = half * 8 + k
                    P.op("pe", lambda e, pt=pt, k=k, kc=kc, q=q: e.transpose(out=pt[:, k, :], in_=M["otm"][q][:, kc * 128:(kc + 1) * 128], identity=ident),
                         reads=[("otm", q), ("ident",)], writes=[("pstp", half)])
                dst = xnT[:, half * 8:(half + 1) * 8, tt * 128:(tt + 1) * 128]
                if half == 0:
                    P.op("act", lambda e, dst=dst, pt=pt: e.copy(out=dst, in_=pt), reads=[("pstp", half)], writes=[("xnT", tt, half)])
                else:
                    P.op("dve", lambda e, dst=dst, pt=pt: e.tensor_copy(out=dst, in_=pt), reads=[("pstp", half)], writes=[("xnT", tt, half)])
        outproj_pass(P, A, ps, ring, rs, xnT, lambda hc, tt: [("xnT", tt, hc // 8)], 16, w_out_v, io["h_own"], p * 1024, 1.0)


def host_consts(half, W):
    import ml_dtypes
    bf = ml_dtypes.bfloat16
    j = np.arange(128)[:, None]
    q = np.arange(128)[None, :]
    triu = (j > q).astype(np.float32)
    tril = (j <= q).astype(np.float32)
    ones = np.ones((128, 128), np.float32)

    def neg4(m):
        a = (1.0 - m) * NEG
        return np.concatenate([a] * 4, axis=-1)

    wmask = np.zeros((16, W + 1, 128, 128), np.float32)
    for i in range(16):
        for w in range(W + 1):
            c = 16 + i - W + w
            if c - 16 * (1 - half) < 0:
                continue
            wmask[i, w] = triu if w == 0 else (tril if w == W else ones)
    out = {"wmask": np.ascontiguousarray(neg4(wmask).transpose(0, 2, 1, 3)).astype(bf), "tril": neg4(tril).astype(bf)}
    cmask = np.zeros((16, 2, 128, 128), np.float32)
    nu = (np.arange(2)[:, None] * 128 + np.arange(128)[None, :])
    for i in range(16):
        tu = 2048 + i * 128 + np.arange(128)
        ok = (nu[:, :, None] <= 254) & (nu[:, :, None] - 128 * (1 - half) >= 0) & (16 * nu[:, :, None] + 31 <= tu[None, None, :])
        cmask[i] = ok
    out["cmask"] = np.ascontiguousarray(neg4(cmask).transpose(0, 2, 1, 3)).astype(bf)
    selA = np.zeros((16, 128, 64), np.float32)
    selC = np.zeros((16, 128, 64), np.float32)
    selV = np.zeros((16, 128, 64), np.float32)
    ju = np.arange(64)[None, :]
    j0 = 32 * (1 - half)
    for i in range(16):
        tu = (2048 + i * 128 + np.arange(128))[:, None]
        cur = tu // 64
        valid = (ju * 64 <= tu) & (ju >= j0)
        forced = ((ju == j0) | (ju == cur) | (ju == cur - 1)) & valid
        selV[i] = valid
        selA[i] = valid & ~forced
        selC[i] = np.where(forced, 1e9, np.where(valid, 0.0, -1e30))
    out.update(selA=selA, selC=selC, selV=selV)
    ov = np.zeros((128, 2, 64), np.float32)
    for ch in range(2):
        n = ch * 128 + np.arange(128)
        cs = (n * 16)[:, None]
        ss = (np.arange(64) * 64)[None, :]
        ov[:, ch, :] = ((cs < ss + 64) & (cs + 32 > ss) & (n[:, None] <= 254))
    out["overlap"] = ov.astype(bf)
    E = np.zeros((64, 32, 128), np.float32)
    for c in range(32):
        for k in range(128):
            E[2 * c + k // 64, c, k] = 1.0
    out["E"] = E.astype(bf)
    out["ident"] = np.eye(128, dtype=np.float32).astype(bf)
    return out


def invf_table(rd):
    half = rd // 2
    inv = (1.0 / (np.float32(500000.0) ** (np.arange(half, dtype=np.float32) * np.float32(2.0 / rd)))).astype(np.float32)
    return np.broadcast_to(inv[None, :], (128, half)).copy()


def mixer_p2(P, A, ps, ring, rs, cfg, io, M3):
    ident = A["ident"]
    M = carve(A, A["z0"], [
        ("xcT", [128, 4, 4096], BF16), ("ctm", [128, 512], BF16, 2), ("w2b", [128, 2, 128], BF16), ("pe32", [32, 128], BF16),
        ("peT", [128, 32], BF16), ("bias", [128, 2], F32), ("hidT", [128, 2, 256], BF16),
        ("sq", [128, 512], F32, 2), ("y", [128, 512], F32, 2), ("ms", [128, 8], F32, 2), ("rt", [128, 4, 64], F32, 2),
        ("ktmp", [128, 128], BF16, 2), ("posi", [128, 32], I32), ("posf", [128, 32], F32), ("ang", [128, 512], F32),
        ("ang2", [128, 512], F32), ("ang3", [128, 512], F32), ("posi2", [128, 512], I32),
        ("cos", [128, 2, 16], F32), ("sin", [128, 2, 16], F32), ("invf", [128, 16], F32), ("kg", [128, 128], F32),
        ("ov", [128, 2, 64], BF16)])
    M["eps"] = A["eps"]
    kcT, vcaug = M3["kcT"], M3["vcaug"]
    P.dma("sp", M["posi"][:, 0:2], io["pos_cmp"], writes=[("pos", "cmp")], lane=("c1",))
    P.dma("sp", M["invf"], io["invf"], writes=[("invf",)], lane=("c2",))
    P.dma("sp", M["kg"], io["kg"].partition_broadcast(128), writes=[("qkgain",)], lane=("c3",))
    P.dma("sp", M["ov"], io["overlap"], writes=[("ov",)], lane=("c4",))
    rope_tables(P, M, M["posi"][:, 0:2], 2, 16, M["invf"], M["cos"], M["sin"], "cmp")
    P.op("dve", lambda e: e.memset(kcT, 0.0), writes=[("kcT",)])
    P.op("dve", lambda e: e.memset(vcaug, 0.0), writes=[("vcaug",)])
    P.op("dve", lambda e: e.memset(vcaug[:, :, :, 128:129], 1.0), reads=[("vcaug",)], writes=[("vcaug",)])
    P.op("dve", lambda e: e.tensor_copy(out=vcaug[:, :, :, 129:193], in_=M["ov"].unsqueeze(2).to_broadcast([128, 2, 4, 64])),
         reads=[("vcaug",), ("ov",)], writes=[("vcaug",)])
    P.op("dve", lambda e: e.memset(M["hidT"], 0.0), writes=[("hidT",)])
    for which in ("k", "v"):
        src = io["kc"] if which == "k" else io["vc"]
        for ut in range(32):
            q = ut % 2
            P.dma("sp", M["ctm"][q], src[ut * 128:(ut + 1) * 128, :], writes=[("ctm", q)], lane=("lc", q))
            pt = ps["tp"][q]
            for g in range(4):
                P.op("pe", lambda e, pt=pt, g=g, q=q: e.transpose(out=pt[:, g, :], in_=M["ctm"][q][:, g * 128:(g + 1) * 128], identity=ident),
                     reads=[("ctm", q), ("ident",)], writes=[("pstp", q)])
            dst = M["xcT"][:, :, ut * 128:(ut + 1) * 128]
            if q == 0:
                P.op("act", lambda e, dst=dst, pt=pt: e.copy(out=dst, in_=pt[:, 0:4, :]), reads=[("pstp", q)], writes=[("xcT", ut)])
            else:
                P.op("dve", lambda e, dst=dst, pt=pt: e.tensor_copy(out=dst, in_=pt[:, 0:4, :]), reads=[("pstp", q)], writes=[("xcT", ut)])
        s, (w1b,) = ring_load(P, ring, rs, [io["w1_" + which].rearrange("l d h -> d l h")])
        P.dma("pool", M["w2b"], io["w2_" + which].rearrange("(hc p) d -> p hc d", p=128), writes=[("w2b",)], lane=("cw2",))
        P.dma("pool", M["pe32"], io["pe_" + which], writes=[("pe32",)], lane=("cpe",))
        pt = ps["tp"][0]
        P.op("pe", lambda e, pt=pt: e.transpose(out=pt[:, 0, 0:32], in_=M["pe32"], identity=ident[0:32, 0:32]), reads=[("pe32",), ("ident",)], writes=[("pstp", 0)])
        P.op("dve", lambda e, pt=pt: e.tensor_copy(out=M["peT"], in_=pt[:, 0, 0:32]), reads=[("pstp", 0)], writes=[("peT",)])
        for hc in range(2):
            pb = ps["g"][hc][:, 0:1]
            for l in range(32):
                P.op("pe", lambda e, pb=pb, l=l, hc=hc, w1b=w1b: e.matmul(pb, lhsT=w1b[:, l, hc * 128:(hc + 1) * 128], rhs=M["peT"][:, l:l + 1], start=(l == 0), stop=(l == 31)),
                     reads=[("ring", s, 0), ("ring", s, 1), ("peT",)], writes=[("psg", hc)])
            P.op("dve", lambda e, pb=pb, hc=hc: e.tensor_copy(out=M["bias"][:, hc:hc + 1], in_=pb), reads=[("psg", hc)], writes=[("bias", hc)])
        xall = [("xcT", ut) for ut in range(32)]
        for g in range(4):
            for hc in range(2):
                ph = ps["u"][hc][:, 0:255]
                for l in range(32):
                    P.op("pe", lambda e, ph=ph, l=l, hc=hc, g=g, w1b=w1b: e.matmul(ph, lhsT=w1b[:, l, hc * 128:(hc + 1) * 128], rhs=M["xcT"][:, g, l:l + 4065:16], start=(l == 0), stop=(l == 31)),
                         reads=[("ring", s, 0), ("ring", s, 1)] + (xall if l == 0 else []), writes=[("psu", hc)])
                P.op("act", lambda e, ph=ph, hc=hc: e.activation(out=M["hidT"][:, hc, 0:255], in_=ph, func=AF.Silu, bias=M["bias"][:, hc:hc + 1]),
                     reads=[("psu", hc), ("bias", hc), ("hidT",)], writes=[("hid", hc)])
            for nch in range(2):
                pz = ps["g"][nch][:, 0:128]
                for hc in range(2):
                    P.op("pe", lambda e, pz=pz, hc=hc, nch=nch: e.matmul(pz, lhsT=M["hidT"][:, hc, nch * 128:(nch + 1) * 128], rhs=M["w2b"][:, hc, :], start=(hc == 0), stop=(hc == 1)),
                         reads=[("hid", 0), ("hid", 1), ("w2b",)], writes=[("psg", nch)])
                if which == "k":
                    kt = M["ktmp"][nch]
                    head_post(P, M, pz, [("psg", nch)], 1, 128, M["kg"], M["cos"][:, nch, :], M["sin"][:, nch, :], 32,
                              kt.rearrange("p (h d) -> p h d", d=128), nch, [("ktmp", nch)])
                    ptt = ps["tp"][1]
                    P.op("pe", lambda e, ptt=ptt, kt=kt: e.transpose(out=ptt[:, 0, :], in_=kt, identity=ident), reads=[("ktmp", nch), ("ident",)], writes=[("pstp", 1)])
                    P.op("dve", lambda e, ptt=ptt, g=g, nch=nch: e.tensor_copy(out=kcT[:, g, nch * 128:(nch + 1) * 128], in_=ptt[:, 0, :]),
                         reads=[("pstp", 1), ("kcT",)], writes=[("kcT",)])
                else:
                    P.op("act", lambda e, pz=pz, g=g, nch=nch: e.copy(out=vcaug[:, nch, g, 0:128], in_=pz), reads=[("psg", nch), ("vcaug",)], writes=[("vcaug",)])


SWA_W = ["w_in", "w_out", "mix_g", "qg", "kg", "sinks"]
NSA_W = ["w_in", "w_out", "mix_g", "qg", "kg", "pe_k", "w1_k", "w2_k", "pe_v", "w1_v", "w2_v"]
W_SHAPES = {
    "ffn": {"w_in": [2048, 11264], "w_out": [5632, 2048], "g": [2048]},
    "swa": {"w_in": [2048, 2560], "w_out": [2048, 2048], "mix_g": [2048], "qg": [64], "kg": [64], "sinks": [32]},
    "nsa": {"w_in": [2048, 5168], "w_out": [2048, 2048], "mix_g": [2048], "qg": [128], "kg": [128],
            "pe_k": [32, 128], "w1_k": [32, 128, 256], "w2_k": [256, 128], "pe_v": [32, 128], "w1_v": [32, 128, 256], "w2_v": [256, 128]},
}
CONST_SHAPES = {
    "pos_T": ([128, 32], I32), "pos_cmp": ([128, 2], I32), "invf8": ([128, 8], F32), "invf16": ([128, 16], F32),
    "wmask1": ([16, 128, 2, 512], BF16), "wmask4": ([16, 128, 5, 512], BF16), "cmask": ([16, 128, 2, 512], BF16),
    "tril": ([128, 512], BF16), "selA": ([16, 128, 64], F32), "selC": ([16, 128, 64], F32), "selV": ([16, 128, 64], F32),
    "overlap": ([128, 2, 64], BF16), "E": ([64, 32, 128], BF16), "ident": ([128, 128], BF16),
}


GROUPS = [[0, 1], [2, 3], [4, 5], [6, 7]]
XCH_ROWS = 256


def build_program(phases):
    import contextlib
    nc = bass.Bass("TRN2", target_bir_lowering=False)

    def inp(name, shape, dt=F32):
        return nc.dram_tensor(name, list(shape), dt, kind="ExternalInput").ap()

    def scratch(name, shape, dt):
        return nc.dram_tensor(name, list(shape), dt, kind="Internal").ap()

    kinds = set(phases)
    C = {}
    need = ["ident"]
    if "swa" in kinds:
        need += ["pos_T", "invf8", "wmask1", "tril"]
    if "nsa" in kinds:
        need += ["pos_T", "pos_cmp", "invf16", "wmask4", "cmask", "tril", "selA", "selC", "selV", "overlap", "E"]
    for n in dict.fromkeys(need):
        C[n] = inp(n, *CONST_SHAPES[n])
    h_in = inp("h_in", [2048, 2048])
    h_out = nc.dram_tensor("h", [2048, 2048], F32, kind="ExternalOutput").ap()
    h = scratch("hw", [2048, 2048], F32)
    nx = 2048 // XCH_ROWS
    hg = scratch("hg", [nx, 2 * XCH_ROWS, 2048], F32)
    W = []
    for p, k in enumerate(phases):
        W.append({n: inp("p%d_%s" % (p, n), shp) for n, shp in W_SHAPES[k].items()})
    S = {}
    if "swa" in kinds:
        S["swa"] = {"Ks": scratch("swa_Ks", [4096, 512], BF16), "Vs": scratch("swa_Vs", [4096, 256], BF16),
                    "Qs": scratch("swa_Qs", [2048, 4096], BF16), "Os": scratch("swa_Os", [2048, 2048], BF16)}
    if "nsa" in kinds:
        S["nsa"] = {n: scratch("nsa_" + n, [4096, 512], BF16) for n in ("kc", "vc", "ks", "vs", "kw", "vw")}
        S["nsa"].update(Qs=scratch("nsa_Qs", [2048, 2048], BF16), Gs=scratch("nsa_Gs", [2048, 48], F32), Os=scratch("nsa_Os", [2048, 2048], BF16))

    def kv_src(r0):
        if r0 >= 2048:
            return h[r0 - 2048:r0 - 2048 + 128, :]
        return hg[r0 // XCH_ROWS, (r0 % XCH_ROWS):(r0 % XCH_ROWS) + 128, :]

    with contextlib.ExitStack() as st:
        A, ps, ring, ar = setup_common(nc, st)
        P = Prog(nc)
        init_consts(P, A, C["ident"])
        for r in range(4):
            P.dma("sp", h[r * 512:(r + 1) * 512, :], h_in[r * 512:(r + 1) * 512, :], writes=[("cp", r)], lane=("cp", r))
        rs = {"n": 0}
        for p, k in enumerate(phases):
            P.barrier()
            nxt = phases[p + 1] if p + 1 < len(phases) else None

            def xch(js):
                for j in js:
                    src = h[j * XCH_ROWS:(j + 1) * XCH_ROWS, :]
                    dst = hg[j]
                    rows = [("hdram", r0, c) for r0 in range(j * XCH_ROWS, (j + 1) * XCH_ROWS, 128) for c in range(4)]
                    P.coll(lambda e, src=src, dst=dst: e.collective_compute("AllGather", ALU.bypass, replica_groups=GROUPS, ins=[src.opt()], outs=[dst.opt()]),
                           reads=rows, writes=[("hg", j)], lane=("xch", j))

            if k == "ffn":
                load_gain(P, A["nbuf"]["gain"], W[p]["g"])
                hook = None
                if nxt == "nsa":
                    hook = lambda tp: xch(range(0, nx // 2)) if tp == 1 else None
                ffn_phase(P, A, ps, h, W[p]["w_in"], W[p]["w_out"], ring, rs, mid_hook=hook)
                if nxt == "nsa":
                    xch(range(nx // 2, nx))
                elif nxt == "swa":
                    xch([nx - 1])
                continue
            cfg = CFG_SWA if k == "swa" else CFG_NSA
            io = dict(W[p])
            io.update(S[k])
            io.update(h_own=h, h_kv=None, kv_src=kv_src, pos_T=C["pos_T"], tril=C["tril"])
            if k == "swa":
                io.update(invf=C["invf8"], wmask=C["wmask1"])
            else:
                io.update(invf=C["invf16"], wmask=C["wmask4"], pos_cmp=C["pos_cmp"], cmask=C["cmask"], selA=C["selA"], selC=C["selC"],
                          selV=C["selV"], overlap=C["overlap"], E=C["E"])
            mixer_p1(P, A, ps, ring, rs, cfg, io)
            P.barrier()
            M = mixer_p3(P, A, ps, ring, rs, cfg, io)
            if k == "nsa":
                mixer_p2(P, A, ps, ring, rs, cfg, io, M)
                P.barrier()
            mixer_p3_run(P, A, ps, ring, rs, cfg, io, M)
            P.barrier()
            mixer_p4(P, A, ps, ring, rs, cfg, io)
        P.barrier()
        fin = []
        for r in range(4):
            fin.append(P.dma("sp", h_out[r * 512:(r + 1) * 512, :], h[r * 512:(r + 1) * 512, :], writes=[("out", r)], lane=("cpo", r)))
        P.emit(final_waits=fin)
    return nc


_PROGS = {}


def _core_consts(positions, c):
    b, half = c // 2, c % 2
    pos = np.asarray(positions[b]).astype(np.int32)
    if half == 1:
        pos_l = pos
    else:
        pos_l = np.concatenate([np.zeros(2048, np.int32), pos[:2048]])
    nu = np.minimum(np.arange(256), 254)
    out = {"pos_T": np.ascontiguousarray(pos_l.reshape(32, 128).T),
           "pos_cmp": np.ascontiguousarray(pos_l[16 * nu + 31].reshape(2, 128).T),
           "invf8": invf_table(16), "invf16": invf_table(32)}
    h1 = host_consts(half, 1)
    h4 = host_consts(half, 4)
    out["wmask1"] = h1["wmask"]
    out["wmask4"] = h4["wmask"]
    out["cmask"] = h4["cmask"]
    for n in ("tril", "selA", "selC", "selV", "overlap", "E", "ident"):
        out[n] = h4[n]
    return out


def _layer_weights(inp, i, kind, which=None):
    j = i // 2
    if kind == "ffn" and which == 1:
        return {"w_in": inp["ffn1_w_in"][i], "w_out": inp["ffn1_w_out"][i], "g": inp["ffn1_norm"][i]}
    if kind == "ffn":
        return {"w_in": inp["ffn2_w_in"][i], "w_out": inp["ffn2_w_out"][i], "g": inp["ffn2_norm"][i]}
    if kind == "swa":
        return {"w_in": inp["swa_w_in"][j], "w_out": inp["swa_w_out"][j], "mix_g": inp["mix_norm"][i], "qg": inp["swa_q_norm"][j],
                "kg": inp["swa_k_norm"][j], "sinks": inp["swa_sinks"][j]}
    return {"w_in": inp["nsa_w_in"][j], "w_out": inp["nsa_w_out"][j], "mix_g": inp["mix_norm"][i], "qg": inp["nsa_q_norm"][j], "kg": inp["nsa_k_norm"][j],
            "pe_k": inp["nsa_cmp_pe_k"][j], "w1_k": inp["nsa_cmp_w1_k"][j], "w2_k": inp["nsa_cmp_w2_k"][j],
            "pe_v": inp["nsa_cmp_pe_v"][j], "w1_v": inp["nsa_cmp_w1_v"][j], "w2_v": inp["nsa_cmp_w2_v"][j]}


def kernel(**inputs):
    inp = {k: np.asarray(v) for k, v in inputs.items()}
    x = inp["x"].astype(np.float32, copy=False)
    positions = inp["positions"]
    depth = 4
    consts = [_core_consts(positions, c) for c in range(8)]
    grp = []
    for i in range(depth):
        grp += [("ffn", i, 1), ("swa" if i % 2 == 0 else "nsa", i, None), ("ffn", i, 2)]
    phases = tuple(g[0] for g in grp)
    if phases not in _PROGS:
        _PROGS[phases] = build_program(phases)
    nc = _PROGS[phases]
    wmaps = {}
    for p, (kind, i, which) in enumerate(grp):
        for n, a in _layer_weights(inp, i, kind, which).items():
            wmaps["p%d_%s" % (p, n)] = np.ascontiguousarray(a, dtype=np.float32)
    need = ["ident", "pos_T", "invf8", "wmask1", "tril", "pos_cmp", "invf16", "wmask4", "cmask", "selA", "selC", "selV", "overlap", "E"]
    in_maps = []
    for c in range(8):
        m = dict(wmaps)
        m["h_in"] = np.ascontiguousarray(x[c // 2, (c % 2) * 2048:(c % 2 + 1) * 2048])
        for n in need:
            m[n] = consts[c][n]
        in_maps.append(m)
    res = run_bass_kernel_spmd(nc, in_maps, core_ids=list(range(8)))
    out = np.empty((4, 4096, 2048), np.float32)
    for c in range(8):
        out[c // 2, (c % 2) * 2048:(c % 2 + 1) * 2048] = np.asarray(res.results[c]["h"])
    return out
```

```python
import numpy as np
import concourse.bass as bass
import concourse.mybir as mybir
from concourse.bass_utils import run_bass_kernel_spmd

F32 = mybir.dt.float32
BF16 = mybir.dt.bfloat16
I32 = mybir.dt.int32
U8 = mybir.dt.uint8
AF = mybir.ActivationFunctionType
ALU = mybir.AluOpType
AX = mybir.AxisListType

D = 2048
DFF = 5632
NTOK = 2048
EPS = 1e-6

ENGS = ("pe", "act", "dve", "pool", "sp")


class Op:
    __slots__ = ("eng", "fn", "deps", "dma_lane", "idx", "sig", "sigval", "pos", "inc", "epoch", "semi")

    def __init__(self, eng, fn, deps, dma_lane):
        self.eng = eng
        self.fn = fn
        self.deps = deps
        self.dma_lane = dma_lane
        self.sig = False
        self.sigval = None
        self.inc = 16


class Prog:
    def __init__(self, nc):
        self.nc = nc
        self.ops = []
        self.q = {e: [] for e in ENGS}
        self.last_w = {}
        self.readers = {}
        self.lanes = {}
        self.bar = None
        self.epoch = 0
        self.bar_pending = {e: False for e in ENGS}
        self.dma_since_bar = []

    def barrier(self):
        drains = []
        for e in ENGS:
            o = self.op(e, lambda eng: eng.drain())
            if e == "sp":
                for d in self.dma_since_bar:
                    o.deps.append(d)
            drains.append(o)
        self.bar = drains
        self.epoch += 1
        self.bar_pending = {e: True for e in ENGS}
        self.dma_since_bar = []
        self.last_w = {}
        self.readers = {}

    def op(self, eng, fn, reads=(), writes=(), lane=None):
        deps = {}
        if self.bar_pending[eng]:
            self.bar_pending[eng] = False
            for d in self.bar:
                deps[("bar", d.eng)] = d

        def add(d):
            if d is None:
                return
            if d.dma_lane is not None:
                deps[("d", d.idx)] = d
            else:
                k = ("e", d.eng)
                if k not in deps or deps[k].pos < d.pos:
                    deps[k] = d

        for r in reads:
            add(self.last_w.get(r))
        for r in writes:
            add(self.last_w.get(r))
            for rd in self.readers.get(r, {}).values():
                add(rd)
        o = Op(eng, fn, list(deps.values()), lane)
        o.epoch = self.epoch
        o.idx = len(self.ops)
        o.pos = len(self.q[eng])
        self.ops.append(o)
        self.q[eng].append(o)
        if lane is not None:
            self.dma_since_bar.append(o)
        for r in writes:
            self.last_w[r] = o
            self.readers[r] = {}
        for r in reads:
            k = ("d", o.idx) if lane is not None else ("e", eng)
            self.readers.setdefault(r, {})[k] = o
        return o

    def coll(self, fn, reads=(), writes=(), lane=None):
        o = self.op("pool", fn, reads, writes, lane)
        o.inc = 1
        return o

    def dma(self, eng, out, in_, reads=(), writes=(), lane=None):
        assert lane is not None
        return self.op(eng, ("dma", out, in_), reads, writes, lane)

    def emit(self, final_waits=()):
        nc = self.nc
        for o in self.ops:
            for d in o.deps:
                if d.dma_lane is not None or d.eng != o.eng or o.eng != "pe":
                    d.sig = True
        for o in final_waits:
            o.sig = True
        for o in self.ops:
            if o.dma_lane is not None:
                o.sig = True
        cnt = {e: 0 for e in ENGS}
        for e in ENGS:
            for o in self.q[e]:
                if o.sig and o.dma_lane is None:
                    cnt[e] += 1
                    o.sigval = cnt[e]
        lane_map = {}
        per_epoch = {}
        pool_cnt = {}
        dma_ops = sorted((o for o in self.ops if o.dma_lane is not None), key=lambda o: (o.epoch, ENGS.index(o.eng), o.pos))
        for o in dma_ops:
            key = (o.epoch, o.dma_lane)
            if key not in lane_map:
                lane_map[key] = per_epoch.get(o.epoch, 0)
                per_epoch[o.epoch] = lane_map[key] + 1
            o.semi = lane_map[key]
            pool_cnt[o.semi] = pool_cnt.get(o.semi, 0) + o.inc
            o.sigval = pool_cnt[o.semi]
        npool = max(per_epoch.values()) if per_epoch else 0
        import contextlib
        with contextlib.ExitStack() as st:
            esem = {e: st.enter_context(nc.semaphore("s_" + e)) for e in ENGS}
            lsem = {i: st.enter_context(nc.semaphore("l_%d" % i)) for i in range(npool)}
            block = st.enter_context(nc.Block())
            engobj = {}

            def run(e, eng):
                waited = {}
                for o in self.q[e]:
                    need = {}
                    for d in o.deps:
                        if d.dma_lane is not None:
                            key = ("l", d.semi)
                        elif d.eng != e or e != "pe":
                            key = ("e", d.eng)
                        else:
                            continue
                        if d.sigval > need.get(key, 0):
                            need[key] = d.sigval
                    for key, v in need.items():
                        if waited.get(key, 0) >= v:
                            continue
                        waited[key] = v
                        sem = lsem[key[1]] if key[0] == "l" else esem[key[1]]
                        eng.wait_ge(sem, v)
                    if isinstance(o.fn, tuple):
                        ins = eng.dma_start(out=o.fn[1], in_=o.fn[2])
                    else:
                        ins = o.fn(eng)
                    if o.sig:
                        if o.dma_lane is not None:
                            ins.then_inc(lsem[o.semi], o.inc)
                        else:
                            ins.then_inc(esem[e], 1)
                if e == "sp":
                    for o in final_waits:
                        sem = lsem[o.semi] if o.dma_lane is not None else esem[o.eng]
                        eng.wait_ge(sem, o.sigval)

            @block.tensor
            def _(eng):
                run("pe", eng)

            @block.scalar
            def _(eng):
                run("act", eng)

            @block.vector
            def _(eng):
                run("dve", eng)

            @block.gpsimd
            def _(eng):
                run("pool", eng)

            @block.sync
            def _(eng):
                run("sp", eng)


class Arena:
    def __init__(self, t, nbytes):
        self.t = t
        self.n = nbytes
        self.off = 0

    def alloc(self, shape, dt, at=None):
        esz = {F32: 4, BF16: 2, I32: 4, U8: 1}[dt]
        n = esz
        for s in shape[1:]:
            n *= s
        if at is None:
            at = (self.off + 31) // 32 * 32
            self.off = at + n
        assert at + n <= self.n, (at, n, self.n)
        v = self.t[0:shape[0], at:at + n].bitcast(dt)
        if len(shape) == 3:
            v = v.rearrange("p (a b) -> p a b", b=shape[2])
        elif len(shape) == 4:
            v = v.rearrange("p (a b c) -> p a b c", b=shape[2], c=shape[3])
        return v, at


NBUF_RES = [("nh", 0), ("nh", 1), ("nh", 2), ("xn", 0), ("xn", 1), ("xn", 2)]
ALIAS_ACT = [("actT", j, n) for j in range(20) for n in range(2)]


def load_gain(P, gain_bc, g_dram):
    P.dma("sp", gain_bc, g_dram.partition_broadcast(128), writes=[("gain",)], lane=("gain",))


def ring_load(P, ring, ring_state, src_list):
    s = ring_state["n"] % len(ring)
    ring_state["n"] += 1
    slot = ring[s]
    off = 0
    views = []
    for i, src in enumerate(src_list):
        shp = src.shape
        n = shp[1] * shp[2]
        dst = slot[:, off:off + n].rearrange("p (a b) -> p a b", b=shp[2])
        if len(src_list) == 1:
            P.dma("pool", dst, src, writes=[("ring", s, 0), ("ring", s, 1)], lane=("ring", s, 0))
        else:
            P.dma("pool", dst, src, reads=[("ringw", s, 1 - i)], writes=[("ring", s, i)], lane=("ring", s, i))
        views.append(dst)
        off += n
    return s, views


def norm_pass(P, A, ps, src, t0, res_fn, ntiles=8, src_fn=None, tiles=None):
    xnT, nbuf, ident = A["xnT"], A["nbuf"], A["ident"]
    gain_bc = nbuf["gain"]
    P.op("dve", lambda e: e.memset(nbuf["ss"][:, 0, 0:1], 0.0), writes=ALIAS_ACT + [("nbuf_fence",)])

    def stage_a(tt):
        r0 = t0 + tt * 128
        q = tt % 3
        hb = nbuf["h"][q]
        sap = src[r0:r0 + 128, :] if src_fn is None else src_fn(r0)
        P.dma("sp", hb, sap, reads=res_fn(r0) + [("nbuf_fence",)], writes=[("nh", q)], lane=("nh", q))
        ss = nbuf["ss"][:, q, 0:1]
        rs = nbuf["ss"][:, q, 1:2]
        xn = nbuf["xn"][q]
        P.op("act", lambda e, hb=hb, ss=ss, xn=xn: e.activation(out=xn, in_=hb, func=AF.Square, accum_out=ss),
             reads=[("nh", q), ("nbuf_fence",)], writes=[("xn", q), ("ss", q)])
        P.op("act", lambda e, ss=ss, rs=rs: e.activation(out=rs, in_=ss, func=AF.Sqrt, bias=A["eps"], scale=1.0 / D),
             reads=[("ss", q), ("eps",)], writes=[("rs", q)])
        P.op("dve", lambda e, rs=rs: e.reciprocal(out=rs, in_=rs), reads=[("rs", q)], writes=[("rs", q)])
        P.op("dve", lambda e, xn=xn, hb=hb, rs=rs: e.scalar_tensor_tensor(out=xn, in0=hb, scalar=rs, in1=gain_bc, op0=ALU.mult, op1=ALU.mult),
             reads=[("nh", q), ("rs", q), ("gain",)], writes=[("xn", q)])

    def stage_b(tt):
        q = tt % 3
        xn = nbuf["xn"][q]
        for half in range(2):
            pt = ps["tp"][half]
            for k in range(8):
                kc = half * 8 + k
                P.op("pe", lambda e, pt=pt, k=k, kc=kc, xn=xn: e.transpose(out=pt[:, k, :], in_=xn[:, kc * 128:(kc + 1) * 128], identity=ident),
                     reads=[("xn", q), ("ident",)], writes=[("pstp", half)])
            dst = xnT[:, half * 8:(half + 1) * 8, tt * 128:(tt + 1) * 128]
            if half == 0:
                P.op("act", lambda e, dst=dst, pt=pt: e.copy(out=dst, in_=pt), reads=[("pstp", half)], writes=[("xnT", tt, half)])
            else:
                P.op("dve", lambda e, dst=dst, pt=pt: e.tensor_copy(out=dst, in_=pt), reads=[("pstp", half)], writes=[("xnT", tt, half)])

    tl = list(range(ntiles) if tiles is None else tiles)
    for idx, tt in enumerate(tl):
        stage_a(tt)
        if idx >= 1:
            stage_b(tl[idx - 1])
    stage_b(tl[-1])


def outproj_pass(P, A, ps, ring, ring_state, srcT, src_res, nk, w_v, h_dram, t0, scale):
    misc = A["misc"]
    banks = [(ps["o"][0], ("pso", 0)), (ps["o"][1], ("pso", 1)), (ps["g"][0], ("psg", 0)), (ps["g"][1], ("psg", 1))]
    pieces = [(k0, min(16, nk - k0)) for k0 in range(0, nk, 16)]
    cnt = 0
    for c in range(D // 512):
        pv = []
        for (h0, hn) in pieces:
            s, (wv,) = ring_load(P, ring, ring_state, [w_v[:, h0:h0 + hn, c * 512:(c + 1) * 512]])
            pv.append((s, wv, h0, hn))
        for grp in range(2):
            for pi, (s, wv, h0, hn) in enumerate(pv):
                for t4 in range(4):
                    tt = grp * 4 + t4
                    po, pres = banks[t4]
                    if pi == 0:
                        r0 = t0 + tt * 128
                        b = (cnt + t4) % 2
                        if t4 < 2:
                            P.dma("sp", misc["hs"][b], h_dram[r0:r0 + 128, c * 512:(c + 1) * 512], reads=[("hdram", r0, c)], writes=[("hs", b)], lane=("hs", b))
                    for hh in range(hn):
                        hc = h0 + hh
                        P.op("pe", lambda e, po=po, wv=wv, hh=hh, hc=hc, tt=tt: e.matmul(po, lhsT=srcT[:, hc, tt * 128:(tt + 1) * 128], rhs=wv[:, hh, :], start=(hc == 0), stop=(hc == nk - 1)),
                             reads=[("ring", s, 0), ("ring", s, 1)] + src_res(hc, tt), writes=[pres])
                    if pi == len(pv) - 1:
                        r0 = t0 + tt * 128
                        b = (cnt + t4) % 2
                        hs, ho = misc["hs"][b], misc["ho"][b]
                        P.op("dve", lambda e, ho=ho, po=po, hs=hs: e.scalar_tensor_tensor(out=ho, in0=po, scalar=scale, in1=hs, op0=ALU.mult, op1=ALU.add),
                             reads=[pres, ("hs", b)], writes=[("ho", b)])
                        if t4 + 2 < 4:
                            r2 = t0 + (tt + 2) * 128
                            P.dma("sp", misc["hs"][b], h_dram[r2:r2 + 128, c * 512:(c + 1) * 512], reads=[("hdram", r2, c)], writes=[("hs", b)], lane=("hs", b))
                        P.dma("sp", h_dram[r0:r0 + 128, c * 512:(c + 1) * 512], ho, reads=[("ho", b)], writes=[("hdram", r0, c)], lane=("ho", b))
            cnt += 4


def ffn_phase(P, A, ps, h_dram, w_in, w_out, ring, ring_state, mid_hook=None):
    TT = 1024
    xnT, actT, misc = A["xnT"], A["actT"], A["misc"]
    w_in_v = w_in.rearrange("(kc p) n -> p kc n", p=128)
    w_out_v = w_out.rearrange("(hc p) n -> p hc n", p=128)
    for tp in range(NTOK // TT):
        t0 = tp * TT
        norm_pass(P, A, ps, h_dram, t0, lambda r0: [("hdram", r0, c) for c in range(4)])
        for jp in range(DFF // 256):
            s, (wg, wu) = ring_load(P, ring, ring_state, [w_in_v[:, :, jp * 256:(jp + 1) * 256], w_in_v[:, :, DFF + jp * 256:DFF + (jp + 1) * 256]])
            for jj in range(2):
                j = jp * 2 + jj
                for n in range(TT // 512):
                    b = (j * 2 + n) % 2
                    pg, pu = ps["g"][b], ps["u"][b]
                    for kc in range(16):
                        P.op("pe", lambda e, pg=pg, wg=wg, kc=kc, jj=jj, n=n: e.matmul(pg, lhsT=wg[:, kc, jj * 128:(jj + 1) * 128], rhs=xnT[:, kc, n * 512:(n + 1) * 512], start=(kc == 0), stop=(kc == 15)),
                             reads=[("ring", s, 0)] + [("xnT", n * 4 + q, kc // 8) for q in range(4)], writes=[("psg", b)])
                    for kc in range(16):
                        P.op("pe", lambda e, pu=pu, wu=wu, kc=kc, jj=jj, n=n: e.matmul(pu, lhsT=wu[:, kc, jj * 128:(jj + 1) * 128], rhs=xnT[:, kc, n * 512:(n + 1) * 512], start=(kc == 0), stop=(kc == 15)),
                             reads=[("ring", s, 1)] + [("xnT", n * 4 + q, kc // 8) for q in range(4)], writes=[("psu", b)])
                    sg = misc["sg"][b]
                    P.op("act", lambda e, sg=sg, pg=pg: e.activation(out=sg, in_=pg, func=AF.Silu), reads=[("psg", b)], writes=[("sg", b)])
                    dst = actT[:, j, n * 512:(n + 1) * 512]
                    P.op("dve", lambda e, dst=dst, sg=sg, pu=pu: e.tensor_tensor(out=dst, in0=sg, in1=pu, op=ALU.mult),
                         reads=[("sg", b), ("psu", b)], writes=[("actT", j, n)] + (NBUF_RES if j < 20 else []))
        if mid_hook is not None:
            mid_hook(tp)
        outproj_pass(P, A, ps, ring, ring_state, actT, lambda hc, tt: [("actT", hc, tt // 4)], 44, w_out_v, h_dram, t0, 0.5)


def init_consts(P, A, identd):
    P.dma("sp", A["ident"], identd, writes=[("ident",)], lane=("const",))
    P.op("dve", lambda e: e.memset(A["eps"], EPS), writes=[("eps",)])


def setup_common(nc, st):
    NB = 212000
    big = st.enter_context(nc.sbuf_tensor("arena", [128, NB], U8))
    ar = Arena(big, NB)
    A = {"big": big, "NB": NB}
    A["ident"], _ = ar.alloc([128, 128], BF16)
    A["eps"], _ = ar.alloc([128, 1], F32)
    misc = {}
    misc["sg"] = [ar.alloc([128, 512], F32)[0] for _ in range(2)]
    misc["hs"] = [ar.alloc([128, 512], F32)[0] for _ in range(2)]
    misc["ho"] = [ar.alloc([128, 512], F32)[0] for _ in range(2)]
    A["misc"] = misc
    ring = []
    for _ in range(4):
        v, at = ar.alloc([128, 8192], BF16)
        ring.append(v)
        if "ring0" not in A:
            A["ring0"] = at
    A["xnT"], A["z0"] = ar.alloc([128, 16, 1024], BF16)
    A["actT"], act_at = ar.alloc([128, 44, 1024], BF16)
    A["act_at"] = act_at
    assert act_at == A["z0"] + 32768
    sub = Arena(big, NB)
    sub.off = act_at
    nbuf = {}
    nbuf["h"] = [sub.alloc([128, 2048], F32)[0] for _ in range(3)]
    nbuf["xn"] = [sub.alloc([128, 2048], BF16)[0] for _ in range(3)]
    nbuf["gain"], _ = ar.alloc([128, 2048], F32)
    nbuf["ss"], _ = sub.alloc([128, 3, 2], F32)
    assert sub.off <= act_at + 20 * 2048
    A["nbuf"] = nbuf
    ps = {}
    ps["tp"] = [st.enter_context(nc.psum_tensor("ps_tp%d" % i, [128, 8, 128], BF16))[:] for i in range(2)]
    ps["g"] = [st.enter_context(nc.psum_tensor("ps_g%d" % i, [128, 512], F32))[:] for i in range(2)]
    ps["u"] = [st.enter_context(nc.psum_tensor("ps_u%d" % i, [128, 512], F32))[:] for i in range(2)]
    ps["o"] = [st.enter_context(nc.psum_tensor("ps_o%d" % i, [128, 512], F32))[:] for i in range(2)]
    return A, ps, ring, ar


CFG_SWA = dict(name="swa", H=32, R=8, dh=64, dv=64, rd=16, W=1, ncol=2560,
               kv_slabs=[("kv", 2048, 512)], nq=4)
CFG_NSA = dict(name="nsa", H=16, R=4, dh=128, dv=128, rd=32, W=4, ncol=5168,
               kv_slabs=[("kc", 2048, 512), ("vc", 2560, 512), ("ks", 3072, 512), ("vs", 3584, 512), ("kw", 4096, 512), ("vw", 4608, 512)], nq=4)
SCALE = {64: 64 ** -0.5, 128: 128 ** -0.5}


def head_post(P, M, pz, pz_res, nh, dh, gain_ap, cos_ap, sin_ap, rd, out_ap, q, out_res, defer=None):
    half = rd // 2
    sq = M["sq"][q][:, 0:nh * dh]
    y = M["y"][q][:, 0:nh * dh]
    ms = M["ms"][q][:, 0:nh]
    R = [("hp", q)]
    y3 = y.rearrange("p (h d) -> p h d", d=dh)
    P.op("act", lambda e: e.copy(out=y, in_=pz), reads=pz_res, writes=[("hp_y", q)])
    P.op("act", lambda e: e.activation(out=sq, in_=y, func=AF.Square), reads=[("hp_y", q)], writes=[("hp_sq", q)])
    P.op("dve", lambda e: e.tensor_reduce(out=ms, in_=sq.rearrange("p (h d) -> p h d", d=dh), axis=AX.X, op=ALU.add),
         reads=[("hp_sq", q)], writes=[("hp_ms", q)])
    def stage2():
        P.op("act", lambda e: e.activation(out=ms, in_=ms, func=AF.Sqrt, bias=M["eps"], scale=1.0 / dh), reads=[("hp_ms", q), ("eps",)], writes=[("hp_ms", q)])
        P.op("dve", lambda e: e.reciprocal(out=ms, in_=ms), reads=[("hp_ms", q)], writes=[("hp_ms", q)])
        P.op("dve", lambda e: e.tensor_tensor(out=y3, in0=y3, in1=ms.unsqueeze(2).to_broadcast([128, nh, dh]), op=ALU.mult),
             reads=[("hp_y", q), ("hp_ms", q)], writes=[("hp_y", q)])
        P.op("dve", lambda e: e.tensor_tensor(out=out_ap[:, :, rd:dh], in0=y3[:, :, rd:dh], in1=gain_ap[:, rd:dh].unsqueeze(1).to_broadcast([128, nh, dh - rd]), op=ALU.mult),
             reads=[("hp_y", q), ("qkgain",)], writes=out_res)
        P.op("dve", lambda e: e.tensor_tensor(out=y3[:, :, 0:rd], in0=y3[:, :, 0:rd], in1=gain_ap[:, 0:rd].unsqueeze(1).to_broadcast([128, nh, rd]), op=ALU.mult),
             reads=[("hp_y", q), ("qkgain",)], writes=[("hp_y", q)])
        t = M["rt"][q]
        cb = cos_ap.unsqueeze(1).to_broadcast([128, nh, half])
        sb = sin_ap.unsqueeze(1).to_broadcast([128, nh, half])
        y1, y2 = y3[:, :, 0:half], y3[:, :, half:rd]
        t1, t2, t3, t4 = [t[:, k, 0:nh * half].rearrange("p (h d) -> p h d", d=half) for k in range(4)]
        P.op("dve", lambda e: e.tensor_tensor(out=t1, in0=y1, in1=cb, op=ALU.mult), reads=[("hp_y", q), ("rope",)], writes=[("hp_t", q)])
        P.op("dve", lambda e: e.tensor_tensor(out=t2, in0=y2, in1=sb, op=ALU.mult), reads=[("hp_y", q), ("rope",)], writes=[("hp_t", q)])
        P.op("dve", lambda e: e.tensor_tensor(out=t3, in0=y2, in1=cb, op=ALU.mult), reads=[("hp_y", q), ("rope",)], writes=[("hp_t", q)])
        P.op("dve", lambda e: e.tensor_tensor(out=t4, in0=y1, in1=sb, op=ALU.mult), reads=[("hp_y", q), ("rope",)], writes=[("hp_t", q)])
        P.op("dve", lambda e: e.tensor_tensor(out=out_ap[:, :, 0:half], in0=t1, in1=t2, op=ALU.subtract), reads=[("hp_t", q)], writes=out_res)
        P.op("dve", lambda e: e.tensor_tensor(out=out_ap[:, :, half:rd], in0=t3, in1=t4, op=ALU.add), reads=[("hp_t", q)], writes=out_res)

    if defer is None:
        stage2()
    else:
        defer.append(stage2)


def rope_tables(P, M, pos_ap, ntile, half, invf_ap, cos_out, sin_out, tag):
    TWO_PI = 6.283185307179586
    posf = M["posf"][:, 0:ntile]
    ang = M["ang"][:, 0:ntile * half].rearrange("p (n d) -> p n d", d=half)
    a2 = M["ang2"][:, 0:ntile * half].rearrange("p (n d) -> p n d", d=half)
    P.op("dve", lambda e: e.tensor_copy(out=posf, in_=pos_ap), reads=[("pos", tag)], writes=[("posf",)])
    P.op("dve", lambda e: e.tensor_tensor(out=ang, in0=posf.unsqueeze(2).to_broadcast([128, ntile, half]),
                                          in1=invf_ap.unsqueeze(1).to_broadcast([128, ntile, half]), op=ALU.mult),
         reads=[("posf",), ("invf",)], writes=[("ang",)])
    ki = M["posi2"][:, 0:ntile * half].rearrange("p (n d) -> p n d", d=half)
    kf = M["ang3"][:, 0:ntile * half].rearrange("p (n d) -> p n d", d=half)

    def reduce_and_sin(src_shift, out_ap):
        P.op("dve", lambda e: e.tensor_scalar(out=a2, in0=ang, scalar1=src_shift, scalar2=None, op0=ALU.add), reads=[("ang",), ("rope",)], writes=[("ang2",)])
        P.op("dve", lambda e: e.tensor_scalar(out=kf, in0=a2, scalar1=1.0 / TWO_PI, scalar2=None, op0=ALU.mult), reads=[("ang2",)], writes=[("kf",)])
        P.op("dve", lambda e: e.tensor_copy(out=ki, in_=kf), reads=[("kf",)], writes=[("ki",)])
        P.op("dve", lambda e: e.tensor_copy(out=kf, in_=ki), reads=[("ki",)], writes=[("kf",)])
        P.op("dve", lambda e: e.scalar_tensor_tensor(out=a2, in0=kf, scalar=-TWO_PI, in1=a2, op0=ALU.mult, op1=ALU.add), reads=[("kf",), ("ang2",)], writes=[("ang2",)])
        P.op("dve", lambda e: e.tensor_scalar(out=kf, in0=a2, scalar1=float(np.pi), scalar2=-TWO_PI, op0=ALU.is_gt, op1=ALU.mult), reads=[("ang2",)], writes=[("kf",)])
        P.op("dve", lambda e: e.tensor_tensor(out=a2, in0=a2, in1=kf, op=ALU.add), reads=[("kf",), ("ang2",)], writes=[("ang2",)])
        P.op("dve", lambda e: e.tensor_scalar(out=a2, in0=a2, scalar1=float(np.pi), scalar2=-float(np.pi), op0=ALU.min, op1=ALU.max), reads=[("ang2",)], writes=[("ang2",)])
        P.op("act", lambda e: e.activation(out=out_ap, in_=a2, func=AF.Sin), reads=[("ang2",)], writes=[("rope",)])

    reduce_and_sin(0.0, sin_out)
    reduce_and_sin(float(np.pi / 2), cos_out)


def carve(A, base, spec):
    ar = Arena(A["big"], A["NB"])
    ar.off = base
    out = {}
    for item in spec:
        name, shape, dt = item[0], item[1], item[2]
        cnt = item[3] if len(item) > 3 else None
        if cnt is None:
            out[name] = ar.alloc(shape, dt)[0]
        else:
            out[name] = [ar.alloc(shape, dt)[0] for _ in range(cnt)]
    out["_end"] = ar.off
    return out


def mixer_p1(P, A, ps, ring, rs, cfg, io):
    nm = cfg["name"]
    dh, rd = cfg["dh"], cfg["rd"]
    half = rd // 2
    nhs = 512 // dh
    xnT = A["xnT"]
    M = carve(A, A["act_at"] + 40960, [
        ("sq", [128, 512], F32, 4), ("y", [128, 512], F32, 4), ("ms", [128, 8], F32, 4), ("rt", [128, 4, 64], F32, 4),
        ("outb", [128, 1024], BF16, 4), ("vout", [128, 512], BF16, 4), ("gout", [128, 48], F32, 4),
        ("posi", [128, 32], I32), ("posf", [128, 32], F32), ("ang", [128, 512], F32), ("ang2", [128, 512], F32), ("ang3", [128, 512], F32), ("posi2", [128, 512], I32),
        ("cos", [128, 32, 16], F32), ("sin", [128, 32, 16], F32), ("invf", [128, 16], F32),
        ("qg", [128, 128], F32), ("kg", [128, 128], F32)])
    assert M["_end"] <= A["act_at"] + 88 * 1024
    M["eps"] = A["eps"]
    P.dma("sp", M["posi"], io["pos_T"], writes=[("pos", "kv")], lane=("c1",))
    P.dma("sp", M["invf"][:, 0:half], io["invf"], writes=[("invf",)], lane=("c2",))
    P.dma("sp", M["qg"][:, 0:dh], io["qg"].partition_broadcast(128), writes=[("qkgain",)], lane=("c3",))
    P.dma("sp", M["kg"][:, 0:dh], io["kg"].partition_broadcast(128), writes=[("qkgain",)], lane=("c4",))
    cosv, sinv = M["cos"][:, :, 0:half], M["sin"][:, :, 0:half]
    rope_tables(P, M, M["posi"], 32, half, M["invf"][:, 0:half], cosv, sinv, "kv")
    for q in range(4):
        P.op("dve", lambda e, q=q: e.memset(M["outb"][q], 0.0), writes=[("outb", q)])
    load_gain(P, A["nbuf"]["gain"], io["mix_g"])
    w_in_v = io["w_in"].rearrange("(kc p) n -> p kc n", p=128)
    cnt = [0]
    pend = [[]]

    def run_pending(dl):
        prev, pend[0] = pend[0], dl
        for f in prev:
            f()

    def proj(w, s, ncols, tt):
        b = cnt[0] % 2
        cnt[0] += 1
        pz = ps["g"][b][:, 0:ncols]
        for kc in range(16):
            P.op("pe", lambda e, pz=pz, w=w, kc=kc, tt=tt: e.matmul(pz, lhsT=xnT[:, kc, tt * 128:(tt + 1) * 128], rhs=w[:, kc, :], start=(kc == 0), stop=(kc == 15)),
                 reads=[("ring", s, 0), ("ring", s, 1), ("xnT", tt, kc // 8)], writes=[("psg", b)])
        return pz, b

    W = cfg["W"]
    for tp in range(4):
        if nm == "swa":
            need = [tt for tt in range(8) if tp * 8 + tt >= 16 - W]
        else:
            need = list(range(8))
        if not need:
            continue
        norm_pass(P, A, ps, io["h_kv"], tp * 1024, lambda r0: [], src_fn=io.get("kv_src"), tiles=need)
        for (kind, c0, ncols) in cfg["kv_slabs"]:
            tts = [tt for tt in need if not (kind in ("kw", "vw") and tp * 8 + tt < 16 - W)]
            if not tts:
                continue
            s, (w,) = ring_load(P, ring, rs, [w_in_v[:, :, c0:c0 + ncols]])
            for tt in tts:
                ut = tp * 8 + tt
                pz, b = proj(w, s, ncols, tt)
                q = cnt[0] % 4
                if kind == "kv":
                    ob = M["outb"][q][:, 0:512].rearrange("p (h d) -> p h d", d=128)
                    dl = []
                    head_post(P, M, pz[:, 0:256], [("psg", b)], 4, 64, M["kg"], cosv[:, ut, :], sinv[:, ut, :], rd, ob, q, [("outb", q)], defer=dl)
                    dl.append(lambda ut=ut, q=q: P.dma("sp", io["Ks"][ut * 128:(ut + 1) * 128, :], M["outb"][q][:, 0:512], reads=[("outb", q)], writes=[("Ks", ut)], lane=("st", q)))
                    run_pending(dl)
                    vo = M["vout"][q][:, 0:256]
                    P.op("act", lambda e, vo=vo, pz=pz: e.copy(out=vo, in_=pz[:, 256:512]), reads=[("psg", b)], writes=[("vout", q)])
                    P.dma("sp", io["Vs"][ut * 128:(ut + 1) * 128, :], vo, reads=[("vout", q)], writes=[("Vs", ut)], lane=("sv", q))
                elif kind in ("ks", "kw"):
                    ob = M["outb"][q][:, 0:512].rearrange("p (h d) -> p h d", d=128)
                    dl = []
                    head_post(P, M, pz, [("psg", b)], 4, 128, M["kg"], cosv[:, ut, :], sinv[:, ut, :], rd, ob, q, [("outb", q)], defer=dl)
                    dl.append(lambda ut=ut, q=q, kind=kind: P.dma("sp", io[kind][ut * 128:(ut + 1) * 128, :], M["outb"][q][:, 0:512], reads=[("outb", q)], writes=[(kind, ut)], lane=("st", q)))
                    run_pending(dl)
                else:
                    vo = M["vout"][q]
                    P.op("act", lambda e, vo=vo, pz=pz: e.copy(out=vo, in_=pz), reads=[("psg", b)], writes=[("vout", q)])
                    P.dma("sp", io[kind][ut * 128:(ut + 1) * 128, :], vo, reads=[("vout", q)], writes=[(kind, ut)], lane=("sv", q))
        if tp >= 2:
            for qs in range(4):
                s, (w,) = ring_load(P, ring, rs, [w_in_v[:, :, qs * 512:(qs + 1) * 512]])
                for tt in range(8):
                    ut = tp * 8 + tt
                    ot = ut - 16
                    pz, b = proj(w, s, 512, tt)
                    q = cnt[0] % 4
                    ob = M["outb"][q][:, 0:nhs * 128].rearrange("p (h d) -> p h d", d=128)
                    dl = []
                    head_post(P, M, pz, [("psg", b)], nhs, dh, M["qg"], cosv[:, ut, :], sinv[:, ut, :], rd, ob, q, [("outb", q)], defer=dl)
                    dl.append(lambda ot=ot, qs=qs, q=q: P.dma("sp", io["Qs"][ot * 128:(ot + 1) * 128, qs * nhs * 128:(qs + 1) * nhs * 128], M["outb"][q][:, 0:nhs * 128],
                                                            reads=[("outb", q)], writes=[("Qs", ot, qs)], lane=("st", q)))
                    run_pending(dl)
            if nm == "nsa":
                s, (w,) = ring_load(P, ring, rs, [w_in_v[:, :, 5120:5168]])
                for tt in range(8):
                    ot = tp * 8 + tt - 16
                    pz, b = proj(w, s, 48, tt)
                    q = cnt[0] % 4
                    go = M["gout"][q]
                    P.op("act", lambda e, go=go, pz=pz: e.activation(out=go, in_=pz, func=AF.Sigmoid), reads=[("psg", b)], writes=[("gout", q)])
                    P.dma("sp", io["Gs"][ot * 128:(ot + 1) * 128, :], go, reads=[("gout", q)], writes=[("Gs", ot)], lane=("sg", q))
    run_pending([])


def mixer_p3(P, A, ps, ring, rs, cfg, io):
    nm = cfg["name"]
    H, R, dv, W = cfg["H"], cfg["R"], cfg["dv"], cfg["W"]
    nsa = nm == "nsa"
    NV = dv + 1
    NVC = 193
    spec = [("qtm", [128, H * 128], BF16, 2), ("qT", [128, H, 128], BF16, 2), ("ktm", [128, W + 1, 512], BF16, 2),
            ("kT", [128, W + 1, 4, 128], BF16, 2), ("vaug", [128, W + 1, 4, NV], BF16, 2), ("wmask", [128, W + 1, 512], BF16, 2),
            ("PT", [128, 16, 4, 128], BF16), ("acc", [128, 4, NVC], F32), ("den", [128, 8], F32), ("wgt", [128, 8], F32),
            ("obf", [128, 2048], BF16, 2), ("esink", [128, 32], F32), ("negtril", [128, 512], BF16)]
    if nsa:
        spec += [("oacc", [128, 2048], F32), ("otmp", [128, 512], F32), ("gates", [128, 48], F32, 2),
                 ("cmask", [128, 2, 512], BF16, 2), ("selA", [128, 64], F32, 2), ("selC", [128, 64], F32, 2), ("selV", [128, 64], F32, 2),
                 ("imp", [128, 64], F32), ("val", [128, 64], F32), ("val2", [128, 64], F32), ("m8", [128, 16], F32),
                 ("selb", [128, 64], BF16), ("nselb", [128, 64], BF16), ("nselT", [64, 4, 512], BF16), ("E", [64, 32, 128], BF16),
                 ("ksT", [128, 4, 4096], BF16), ("vsb", [128, 8, NV], BF16, 2), ("ktm2", [128, 512], BF16, 4)]
    M = carve(A, A["ring0"], spec)
    top = A["act_at"] + 80 * 1024
    assert M["_end"] <= top, (M["_end"], top)
    if nsa:
        M.update({k: v for k, v in carve(A, top, [("kcT", [128, 4, 256], BF16), ("vcaug", [128, 2, 4, NVC], BF16)]).items() if k != "_end"})
    M["ptcnt"] = [0]
    M["ucnt"] = [0]
    M["qpar"] = [0]
    return M


NEG = -30000.0


def attend(P, M, ps, cfg, tag, g, hq, chunks, kT_fn, v_fn, mask_fn, nv, pre_batch=None):
    qT = M["qTcur"]
    scale = SCALE[128] if cfg["dh"] == 128 else SCALE[64]
    acc = M["acc"]
    nb = (len(chunks) + 7) // 8
    for bi in range(nb):
        cb = chunks[bi * 8:(bi + 1) * 8]
        if pre_batch is not None:
            pre_batch(bi, cb)
        par = M["ptcnt"][0] % 2
        M["ptcnt"][0] += 1
        for k, ci in enumerate(cb):
            b = M["ucnt"][0] % 2
            M["ucnt"][0] += 1
            sT = ps["u"][b]
            kap, kres = kT_fn(ci)
            adds = mask_fn(ci)
            P.op("pe", lambda e, sT=sT, kap=kap, last=(len(adds) == 0): e.matmul(sT, lhsT=kap, rhs=qT[:, hq:hq + 4, :], start=True, stop=last),
                 reads=kres + [("qT", M["qpar"][0])], writes=[("psu", b)])
            for ai, (la, ra, rres) in enumerate(adds):
                P.op("pe", lambda e, sT=sT, la=la, ra=ra, last=(ai == len(adds) - 1): e.matmul(sT, lhsT=la, rhs=ra, start=False, stop=last),
                     reads=rres, writes=[("psu", b)])
            slot = par * 8 + k
            pt = M["PT"][:, slot]
            P.op("act", lambda e, pt=pt, sT=sT: e.activation(out=pt, in_=sT.rearrange("p (h t) -> p h t", t=128), func=AF.Exp, scale=scale),
                 reads=[("psu", b)], writes=[("PT", slot)])
        for hh in range(4):
            po = ps["o"][hh // 2][:, 0:2 * nv].rearrange("p (h e) -> p h e", e=nv)[:, hh % 2, :]
            for k, ci in enumerate(cb):
                vap, vres = v_fn(ci)
                slot = par * 8 + k
                P.op("pe", lambda e, po=po, slot=slot, hh=hh, vap=vap, k=k: e.matmul(po, lhsT=M["PT"][:, slot, hh, :], rhs=vap, start=(k == 0), stop=(k == len(cb) - 1)),
                     reads=[("PT", slot)] + vres, writes=[("pso", hh // 2)])
        for pb in range(2):
            src = ps["o"][pb][:, 0:2 * nv].rearrange("p (h e) -> p h e", e=nv)
            dst = acc[:, 2 * pb:2 * pb + 2, 0:nv]
            if bi == 0:
                P.op("dve", lambda e, dst=dst, src=src: e.tensor_copy(out=dst, in_=src), reads=[("pso", pb)], writes=[("acc",)])
            else:
                P.op("dve", lambda e, dst=dst, src=src: e.tensor_tensor(out=dst, in0=dst, in1=src, op=ALU.add), reads=[("pso", pb), ("acc",)], writes=[("acc",)])


def mixer_p3_run(P, A, ps, ring, rs, cfg, io, M):
    nm = cfg["name"]
    H, R, dv, W = cfg["H"], cfg["R"], cfg["dv"], cfg["W"]
    nsa = nm == "nsa"
    NV = dv + 1
    ident = A["ident"]
    Kw = io["kw"] if nsa else io["Ks"]
    Vw = io["vw"] if nsa else io["Vs"]
    P.dma("sp", M["negtril"], io["tril"], writes=[("negtril",)], lane=("c1",))
    for q in range(2):
        P.op("dve", lambda e, q=q: e.memset(M["vaug"][q], 1.0), writes=[("vaug_init", q)])
    if not nsa:
        P.dma("sp", M["esink"], io["sinks"].partition_broadcast(128), writes=[("esink",)], lane=("c2",))
        P.op("act", lambda e: e.activation(out=M["esink"], in_=M["esink"], func=AF.Exp), reads=[("esink",)], writes=[("esink",)])
    else:
        P.dma("sp", M["E"], io["E"], writes=[("E",)], lane=("c2",))
        for q in range(2):
            P.op("dve", lambda e, q=q: e.memset(M["vsb"][q], 1.0), writes=[("vsb_init", q)])
        for ut in range(32):
            q = ut % 4
            pq = ut % 2
            P.dma("sp", M["ktm2"][q], io["ks"][ut * 128:(ut + 1) * 128, :], writes=[("ktm2", q)], lane=("lk2", q))
            pt = ps["tp"][pq]
            for g in range(4):
                P.op("pe", lambda e, pt=pt, g=g, q=q: e.transpose(out=pt[:, g, :], in_=M["ktm2"][q][:, g * 128:(g + 1) * 128], identity=ident),
                     reads=[("ktm2", q), ("ident",)], writes=[("pstp", pq)])
            dst = M["ksT"][:, :, ut * 128:(ut + 1) * 128]
            if pq == 0:
                P.op("act", lambda e, dst=dst, pt=pt: e.copy(out=dst, in_=pt[:, 0:4, :]), reads=[("pstp", pq)], writes=[("ksT", ut)])
            else:
                P.op("dve", lambda e, dst=dst, pt=pt: e.tensor_copy(out=dst, in_=pt[:, 0:4, :]), reads=[("pstp", pq)], writes=[("ksT", ut)])

    def tile_loads(i):
        par = i % 2
        ut = 16 + i
        c0 = ut - W
        P.dma("sp", M["qtm"][par], io["Qs"][i * 128:(i + 1) * 128, :], writes=[("qtm", par)], lane=("lq", par))
        P.dma("sp", M["wmask"][par], io["wmask"][i], writes=[("wmask", par)], lane=("lm", par))
        P.dma("sp", M["ktm"][par], Kw[c0 * 128:(ut + 1) * 128, :].rearrange("(w p) c -> p w c", p=128), writes=[("ktm", par)], lane=("lk", par))
        for w in range(W + 1):
            P.dma("sp", M["vaug"][par][:, w, :, 0:dv], Vw[(c0 + w) * 128:(c0 + w + 1) * 128, :].rearrange("p (g d) -> p g d", d=dv),
                  reads=[("vaug_init", par)], writes=[("vaug", par, w)], lane=("lv", par, w))
        if nsa:
            P.dma("sp", M["gates"][par], io["Gs"][i * 128:(i + 1) * 128, :], writes=[("gates", par)], lane=("lg", par))
            P.dma("sp", M["cmask"][par], io["cmask"][i], writes=[("cmask", par)], lane=("lm2", par))
            P.dma("sp", M["selA"][par], io["selA"][i], writes=[("selA", par)], lane=("ls1", par))
            P.dma("sp", M["selC"][par], io["selC"][i], writes=[("selC", par)], lane=("ls2", par))
            P.dma("sp", M["selV"][par], io["selV"][i], writes=[("selV", par)], lane=("ls3", par))

    vsb_cnt = [0]
    tile_loads(0)
    for i in range(16):
        ut = 16 + i
        par = i % 2
        for hb in range(H // 8):
            pt = ps["tp"][hb % 2]
            for k in range(8):
                hd = hb * 8 + k
                P.op("pe", lambda e, pt=pt, k=k, hd=hd, par=par: e.transpose(out=pt[:, k, :], in_=M["qtm"][par][:, hd * 128:(hd + 1) * 128], identity=ident),
                     reads=[("qtm", par), ("ident",)], writes=[("pstp", hb % 2)])
            dst = M["qT"][par][:, hb * 8:(hb + 1) * 8, :]
            if hb % 2 == 0:
                P.op("act", lambda e, dst=dst, pt=pt: e.copy(out=dst, in_=pt), reads=[("pstp", hb % 2)], writes=[("qT", par)])
            else:
                P.op("dve", lambda e, dst=dst, pt=pt: e.tensor_copy(out=dst, in_=pt), reads=[("pstp", hb % 2)], writes=[("qT", par)])
        for w in range(W + 1):
            pt = ps["tp"][w % 2]
            for g in range(4):
                P.op("pe", lambda e, pt=pt, g=g, w=w, par=par: e.transpose(out=pt[:, g, :], in_=M["ktm"][par][:, w, g * 128:(g + 1) * 128], identity=ident),
                     reads=[("ktm", par), ("ident",)], writes=[("pstp", w % 2)])
            dst = M["kT"][par][:, w]
            if w % 2 == 0:
                P.op("act", lambda e, dst=dst, pt=pt: e.copy(out=dst, in_=pt[:, 0:4, :]), reads=[("pstp", w % 2)], writes=[("kT", par)])
            else:
                P.op("dve", lambda e, dst=dst, pt=pt: e.tensor_copy(out=dst, in_=pt[:, 0:4, :]), reads=[("pstp", w % 2)], writes=[("kT", par)])
        if i + 1 < 16:
            tile_loads(i + 1)
        M["qTcur"] = M["qT"][par]
        M["qpar"][0] = par
        obf = M["obf"][par]
        for g in range(4):
            for qd in range(R // 4):
                hq = g * R + qd * 4
                den, wgt = M["den"][:, 0:4], M["wgt"][:, 0:4]
                branches = ["cmp", "win", "sel"] if nsa else ["win"]
                for br, bname in enumerate(branches):
                    if bname == "win":
                        attend(P, M, ps, cfg, bname, g, hq, list(range(W + 1)),
                               lambda ci: (M["kT"][par][:, ci, g, :], [("kT", par)]),
                               lambda ci: (M["vaug"][par][:, ci, g, :], [("vaug", par, ci)]),
                               lambda ci: [(ident, M["wmask"][par][:, ci, :], [("wmask", par), ("ident",)])], NV)
                    elif bname == "cmp":
                        attend(P, M, ps, cfg, bname, g, hq, [0, 1],
                               lambda ci: (M["kcT"][:, g, ci * 128:(ci + 1) * 128], [("kcT",)]),
                               lambda ci: (M["vcaug"][:, ci, g, :], [("vcaug",)]),
                               lambda ci: [(ident, M["cmask"][par][:, ci, :], [("cmask", par), ("ident",)])], 193)
                    else:
                        nch = ut + 1
                        vslot = {}

                        def pre_batch(bi, cb, g=g, vslot=vslot):
                            q = vsb_cnt[0] % 2
                            vsb_cnt[0] += 1
                            for k, ck in enumerate(cb):
                                vslot[ck] = (q, k)
                                P.dma("sp", M["vsb"][q][:, k, 0:dv], io["vs"][ck * 128:(ck + 1) * 128, g * dv:(g + 1) * dv],
                                      reads=[("vsb_init", q)], writes=[("vsb", q, k)], lane=("lvs", q, k))

                        def sel_mask(ci, g=g, ut=ut):
                            adds = [(M["E"][:, ci, :], M["nselT"][:, g, :], [("E",), ("nselT", g)])]
                            if ci == ut:
                                adds.append((ident, M["negtril"], [("negtril",), ("ident",)]))
                            return adds
                        attend(P, M, ps, cfg, bname, g, hq, list(range(nch)),
                               lambda ci: (M["ksT"][:, g, ci * 128:(ci + 1) * 128], [("ksT", ci)]),
                               lambda ci: (M["vsb"][vslot[ci][0]][:, vslot[ci][1], :], [("vsb", vslot[ci][0], vslot[ci][1])]),
                               sel_mask, NV, pre_batch=pre_batch)
                    acc = M["acc"]
                    dcol = acc[:, :, dv]
                    if not nsa:
                        P.op("dve", lambda e, dcol=dcol, hq=hq: e.tensor_tensor(out=den, in0=dcol, in1=M["esink"][:, hq:hq + 4], op=ALU.add),
                             reads=[("acc",), ("esink",)], writes=[("den",)])
                    else:
                        P.op("dve", lambda e, dcol=dcol: e.tensor_scalar(out=den, in0=dcol, scalar1=1e-30, scalar2=None, op0=ALU.max),
                             reads=[("acc",)], writes=[("den",)])
                    P.op("dve", lambda e: e.reciprocal(out=den, in_=den), reads=[("den",)], writes=[("den",)])
                    if not nsa:
                        dst = obf.rearrange("p (h d) -> p h d", d=dv)[:, hq:hq + 4, :]
                        P.op("dve", lambda e, dst=dst: e.tensor_tensor(out=dst, in0=acc[:, :, 0:dv], in1=den.unsqueeze(2).to_broadcast([128, 4, dv]), op=ALU.mult),
                             reads=[("acc",), ("den",)], writes=[("obf", par)])
                        continue
                    gi = {"cmp": 0, "sel": 1, "win": 2}[bname]
                    gsl = M["gates"][par][:, gi * 16 + hq:gi * 16 + hq + 4]
                    P.op("dve", lambda e, gsl=gsl: e.tensor_tensor(out=wgt, in0=den, in1=gsl, op=ALU.mult), reads=[("den",), ("gates", par)], writes=[("wgt",)])
                    oa = M["oacc"].rearrange("p (h d) -> p h d", d=dv)[:, hq:hq + 4, :]
                    if br == 0:
                        P.op("dve", lambda e, oa=oa: e.tensor_tensor(out=oa, in0=acc[:, :, 0:dv], in1=wgt.unsqueeze(2).to_broadcast([128, 4, dv]), op=ALU.mult),
                             reads=[("acc",), ("wgt",)], writes=[("oacc",)])
                        imp = M["imp"]
                        for hh in range(4):
                            if hh == 0:
                                P.op("dve", lambda e: e.tensor_scalar(out=imp, in0=acc[:, 0, 129:193], scalar1=den[:, 0:1], scalar2=None, op0=ALU.mult),
                                     reads=[("acc",), ("den",)], writes=[("imp",)])
                            else:
                                P.op("dve", lambda e, hh=hh: e.scalar_tensor_tensor(out=imp, in0=acc[:, hh, 129:193], scalar=den[:, hh:hh + 1], in1=imp, op0=ALU.mult, op1=ALU.add),
                                     reads=[("acc",), ("den",), ("imp",)], writes=[("imp",)])
                        val, val2, m8 = M["val"], M["val2"], M["m8"]
                        sA, sC, sV = M["selA"][par], M["selC"][par], M["selV"][par]
                        P.op("dve", lambda e, sA=sA: e.tensor_tensor(out=val, in0=imp, in1=sA, op=ALU.mult), reads=[("imp",), ("selA", par)], writes=[("val",)])
                        P.op("dve", lambda e, sC=sC: e.tensor_tensor(out=val, in0=val, in1=sC, op=ALU.add), reads=[("val",), ("selC", par)], writes=[("val",)])
                        P.op("dve", lambda e: e.max(out=m8[:, 0:8], in_=val), reads=[("val",)], writes=[("m8",)])
                        P.op("dve", lambda e: e.match_replace(out=val2, in_to_replace=m8[:, 0:8], in_values=val, imm_value=-3.0e38), reads=[("val",), ("m8",)], writes=[("val2",)])
                        P.op("dve", lambda e: e.max(out=m8[:, 8:16], in_=val2), reads=[("val2",)], writes=[("m8",)])
                        P.op("dve", lambda e: e.tensor_scalar(out=val2, in0=val, scalar1=m8[:, 15:16], scalar2=None, op0=ALU.is_ge), reads=[("val",), ("m8",)], writes=[("val2",)])
                        P.op("dve", lambda e, sV=sV: e.tensor_tensor(out=M["selb"], in0=val2, in1=sV, op=ALU.mult), reads=[("val2",), ("selV", par)], writes=[("selb",)])
                        if "dbg_sel" in io:
                            P.dma("sp", io["dbg_sel"][i, g], M["selb"], reads=[("selb",)], writes=[("dbgsel", i, g)], lane=("dbg1",))
                            P.dma("sp", io["dbg_imp"][i, g], M["imp"], reads=[("imp",)], writes=[("dbgimp", i, g)], lane=("dbg2",))
                            P.dma("sp", io["dbg_val"][i, g], M["val"], reads=[("val",)], writes=[("dbgval", i, g)], lane=("dbg3",))
                            P.dma("sp", io["dbg_m8"][i, g], M["m8"], reads=[("m8",)], writes=[("dbgm8", i, g)], lane=("dbg4",))
                        P.op("dve", lambda e: e.tensor_scalar(out=M["nselb"], in0=M["selb"], scalar1=-NEG, scalar2=NEG, op0=ALU.mult, op1=ALU.add),
                             reads=[("selb",)], writes=[("nselb",)])
                        pt = ps["tp"][0]
                        P.op("pe", lambda e, pt=pt: e.transpose(out=pt[0:64, 0, :], in_=M["nselb"], identity=ident), reads=[("nselb",), ("ident",)], writes=[("pstp", 0)])
                        P.op("dve", lambda e, pt=pt, g=g: e.tensor_copy(out=M["nselT"][:, g, :].rearrange("p (h t) -> p h t", t=128),
                                                                        in_=pt[0:64, 0, :].unsqueeze(1).to_broadcast([64, 4, 128])),
                             reads=[("pstp", 0)], writes=[("nselT", g)])
                    else:
                        ot = M["otmp"].rearrange("p (h d) -> p h d", d=dv)
                        P.op("dve", lambda e, ot=ot: e.tensor_tensor(out=ot, in0=acc[:, :, 0:dv], in1=wgt.unsqueeze(2).to_broadcast([128, 4, dv]), op=ALU.mult),
                             reads=[("acc",), ("wgt",)], writes=[("otmp",)])
                        P.op("dve", lambda e, oa=oa, ot=ot: e.tensor_tensor(out=oa, in0=oa, in1=ot, op=ALU.add), reads=[("otmp",), ("oacc",)], writes=[("oacc",)])
        if nsa:
            P.op("act", lambda e, obf=obf: e.copy(out=obf, in_=M["oacc"]), reads=[("oacc",)], writes=[("obf", par)])
        P.dma("sp", io["Os"][i * 128:(i + 1) * 128, :], obf, reads=[("obf", par)], writes=[("Os", i)], lane=("so", par))


def mixer_p4(P, A, ps, ring, rs, cfg, io):
    xnT, ident = A["xnT"], A["ident"]
    M = carve(A, A["act_at"], [("otm", [128, 2048], BF16, 4)])
    w_out_v = io["w_out"].rearrange("(kc p) n -> p kc n", p=128)
    for p in range(2):
        for tt in range(8):
            i = p * 8 + tt
            q = tt % 4
            P.dma("sp", M["otm"][q], io["Os"][i * 128:(i + 1) * 128, :], writes=[("otm", q)], lane=("lo", q))
            for half in range(2):
                pt = ps["tp"][half]
                for k in range(8):
                    kc = half * 8 + k
                    P.op("pe", lambda e, pt=pt, k=k, kc=kc, q=q: e.transpose(out=pt[:, k, :], in_=M["otm"][q][:, kc * 128:(kc + 1) * 128], identity=ident),
                         reads=[("otm", q), ("ident",)], writes=[("pstp", half)])
                dst = xnT[:, half * 8:(half + 1) * 8, tt * 128:(tt + 1) * 128]
                if half == 0:
                    P.op("act", lambda e, dst=dst, pt=pt: e.copy(out=dst, in_=pt), reads=[("pstp", half)], writes=[("xnT", tt, half)])
                else:
                    P.op("dve", lambda e, dst=dst, pt=pt: e.tensor_copy(out=dst, in_=pt), reads=[("pstp", half)], writes=[("xnT", tt, half)])
        outproj_pass(P, A, ps, ring, rs, xnT, lambda hc, tt: [("xnT", tt, hc // 8)], 16, w_out_v, io["h_own"], p * 1024, 1.0)


def host_consts(half, W):
    import ml_dtypes
    bf = ml_dtypes.bfloat16
    j = np.arange(128)[:, None]
    q = np.arange(128)[None, :]
    triu = (j > q).astype(np.float32)
    tril = (j <= q).astype(np.float32)
    ones = np.ones((128, 128), np.float32)

    def neg4(m):
        a = (1.0 - m) * NEG
        return np.concatenate([a] * 4, axis=-1)

    wmask = np.zeros((16, W + 1, 128, 128), np.float32)
    for i in range(16):
        for w in range(W + 1):
            c = 16 + i - W + w
            if c - 16 * (1 - half) < 0:
                continue
            wmask[i, w] = triu if w == 0 else (tril if w == W else ones)
    out = {"wmask": np.ascontiguousarray(neg4(wmask).transpose(0, 2, 1, 3)).astype(bf), "tril": neg4(tril).astype(bf)}
    cmask = np.zeros((16, 2, 128, 128), np.float32)
    nu = (np.arange(2)[:, None] * 128 + np.arange(128)[None, :])
    for i in range(16):
        tu = 2048 + i * 128 + np.arange(128)
        ok = (nu[:, :, None] <= 254) & (nu[:, :, None] - 128 * (1 - half) >= 0) & (16 * nu[:, :, None] + 31 <= tu[None, None, :])
        cmask[i] = ok
    out["cmask"] = np.ascontiguousarray(neg4(cmask).transpose(0, 2, 1, 3)).astype(bf)
    selA = np.zeros((16, 128, 64), np.float32)
    selC = np.zeros((16, 128, 64), np.float32)
    selV = np.zeros((16, 128, 64), np.float32)
    ju = np.arange(64)[None, :]
    j0 = 32 * (1 - half)
    for i in range(16):
        tu = (2048 + i * 128 + np.arange(128))[:, None]
        cur = tu // 64
        valid = (ju * 64 <= tu) & (ju >= j0)
        forced = ((ju == j0) | (ju == cur) | (ju == cur - 1)) & valid
        selV[i] = valid
        selA[i] = valid & ~forced
        selC[i] = np.where(forced, 1e9, np.where(valid, 0.0, -1e30))
    out.update(selA=selA, selC=selC, selV=selV)
    ov = np.zeros((128, 2, 64), np.float32)
    for ch in range(2):
        n = ch * 128 + np.arange(128)
        cs = (n * 16)[:, None]
        ss = (np.arange(64) * 64)[None, :]
        ov[:, ch, :] = ((cs < ss + 64) & (cs + 32 > ss) & (n[:, None] <= 254))
    out["overlap"] = ov.astype(bf)
    E = np.zeros((64, 32, 128), np.float32)
    for c in range(32):
        for k in range(128):
            E[2 * c + k // 64, c, k] = 1.0
    out["E"] = E.astype(bf)
    out["ident"] = np.eye(128, dtype=np.float32).astype(bf)
    return out


def invf_table(rd):
    half = rd // 2
    inv = (1.0 / (np.float32(500000.0) ** (np.arange(half, dtype=np.float32) * np.float32(2.0 / rd)))).astype(np.float32)
    return np.broadcast_to(inv[None, :], (128, half)).copy()


def mixer_p2(P, A, ps, ring, rs, cfg, io, M3):
    ident = A["ident"]
    M = carve(A, A["z0"], [
        ("xcT", [128, 4, 4096], BF16), ("ctm", [128, 512], BF16, 4), ("w2b", [128, 2, 128], BF16), ("pe32", [32, 128], BF16),
        ("peT", [128, 32], BF16), ("bias", [128, 2], F32), ("hidT", [128, 2, 256], BF16),
        ("sq", [128, 512], F32, 2), ("y", [128, 512], F32, 2), ("ms", [128, 8], F32, 2), ("rt", [128, 4, 64], F32, 2),
        ("ktmp", [128, 128], BF16, 2), ("posi", [128, 32], I32), ("posf", [128, 32], F32), ("ang", [128, 512], F32),
        ("ang2", [128, 512], F32), ("ang3", [128, 512], F32), ("posi2", [128, 512], I32),
        ("cos", [128, 2, 16], F32), ("sin", [128, 2, 16], F32), ("invf", [128, 16], F32), ("kg", [128, 128], F32),
        ("ov", [128, 2, 64], BF16)])
    M["eps"] = A["eps"]
    kcT, vcaug = M3["kcT"], M3["vcaug"]
    P.dma("sp", M["posi"][:, 0:2], io["pos_cmp"], writes=[("pos", "cmp")], lane=("c1",))
    P.dma("sp", M["invf"], io["invf"], writes=[("invf",)], lane=("c2",))
    P.dma("sp", M["kg"], io["kg"].partition_broadcast(128), writes=[("qkgain",)], lane=("c3",))
    P.dma("sp", M["ov"], io["overlap"], writes=[("ov",)], lane=("c4",))
    rope_tables(P, M, M["posi"][:, 0:2], 2, 16, M["invf"], M["cos"], M["sin"], "cmp")
    P.op("dve", lambda e: e.memset(kcT, 0.0), writes=[("kcT",)])
    P.op("dve", lambda e: e.memset(vcaug, 0.0), writes=[("vcaug",)])
    P.op("dve", lambda e: e.memset(vcaug[:, :, :, 128:129], 1.0), reads=[("vcaug",)], writes=[("vcaug",)])
    P.op("dve", lambda e: e.tensor_copy(out=vcaug[:, :, :, 129:193], in_=M["ov"].unsqueeze(2).to_broadcast([128, 2, 4, 64])),
         reads=[("vcaug",), ("ov",)], writes=[("vcaug",)])
    P.op("dve", lambda e: e.memset(M["hidT"], 0.0), writes=[("hidT",)])
    for which in ("k", "v"):
        src = io["kc"] if which == "k" else io["vc"]
        for ut in range(32):
            q = ut % 4
            pq = ut % 2
            P.dma("sp", M["ctm"][q], src[ut * 128:(ut + 1) * 128, :], writes=[("ctm", q)], lane=("lc", q))
            pt = ps["tp"][pq]
            for g in range(4):
                P.op("pe", lambda e, pt=pt, g=g, q=q: e.transpose(out=pt[:, g, :], in_=M["ctm"][q][:, g * 128:(g + 1) * 128], identity=ident),
                     reads=[("ctm", q), ("ident",)], writes=[("pstp", pq)])
            dst = M["xcT"][:, :, ut * 128:(ut + 1) * 128]
            if pq == 0:
                P.op("act", lambda e, dst=dst, pt=pt: e.copy(out=dst, in_=pt[:, 0:4, :]), reads=[("pstp", pq)], writes=[("xcT", ut)])
            else:
                P.op("dve", lambda e, dst=dst, pt=pt: e.tensor_copy(out=dst, in_=pt[:, 0:4, :]), reads=[("pstp", pq)], writes=[("xcT", ut)])
        s, (w1b,) = ring_load(P, ring, rs, [io["w1_" + which].rearrange("l d h -> d l h")])
        P.dma("pool", M["w2b"], io["w2_" + which].rearrange("(hc p) d -> p hc d", p=128), writes=[("w2b",)], lane=("cw2",))
        P.dma("pool", M["pe32"], io["pe_" + which], writes=[("pe32",)], lane=("cpe",))
        pt = ps["tp"][0]
        P.op("pe", lambda e, pt=pt: e.transpose(out=pt[:, 0, 0:32], in_=M["pe32"], identity=ident[0:32, 0:32]), reads=[("pe32",), ("ident",)], writes=[("pstp", 0)])
        P.op("dve", lambda e, pt=pt: e.tensor_copy(out=M["peT"], in_=pt[:, 0, 0:32]), reads=[("pstp", 0)], writes=[("peT",)])
        for hc in range(2):
            pb = ps["g"][hc][:, 0:1]
            for l in range(32):
                P.op("pe", lambda e, pb=pb, l=l, hc=hc, w1b=w1b: e.matmul(pb, lhsT=w1b[:, l, hc * 128:(hc + 1) * 128], rhs=M["peT"][:, l:l + 1], start=(l == 0), stop=(l == 31)),
                     reads=[("ring", s, 0), ("ring", s, 1), ("peT",)], writes=[("psg", hc)])
            P.op("dve", lambda e, pb=pb, hc=hc: e.tensor_copy(out=M["bias"][:, hc:hc + 1], in_=pb), reads=[("psg", hc)], writes=[("bias", hc)])
        xall = [("xcT", ut) for ut in range(32)]
        for g in range(4):
            for hc in range(2):
                ph = ps["u"][hc][:, 0:255]
                for l in range(32):
                    P.op("pe", lambda e, ph=ph, l=l, hc=hc, g=g, w1b=w1b: e.matmul(ph, lhsT=w1b[:, l, hc * 128:(hc + 1) * 128], rhs=M["xcT"][:, g, l:l + 4065:16], start=(l == 0), stop=(l == 31)),
                         reads=[("ring", s, 0), ("ring", s, 1)] + (xall if l == 0 else []), writes=[("psu", hc)])
                P.op("act", lambda e, ph=ph, hc=hc: e.activation(out=M["hidT"][:, hc, 0:255], in_=ph, func=AF.Silu, bias=M["bias"][:, hc:hc + 1]),
                     reads=[("psu", hc), ("bias", hc), ("hidT",)], writes=[("hid", hc)])
            for nch in range(2):
                pz = ps["g"][nch][:, 0:128]
                for hc in range(2):
                    P.op("pe", lambda e, pz=pz, hc=hc, nch=nch: e.matmul(pz, lhsT=M["hidT"][:, hc, nch * 128:(nch + 1) * 128], rhs=M["w2b"][:, hc, :], start=(hc == 0), stop=(hc == 1)),
                         reads=[("hid", 0), ("hid", 1), ("w2b",)], writes=[("psg", nch)])
                if which == "k":
                    kt = M["ktmp"][nch]
                    head_post(P, M, pz, [("psg", nch)], 1, 128, M["kg"], M["cos"][:, nch, :], M["sin"][:, nch, :], 32,
                              kt.rearrange("p (h d) -> p h d", d=128), nch, [("ktmp", nch)])
                    ptt = ps["tp"][1]
                    P.op("pe", lambda e, ptt=ptt, kt=kt: e.transpose(out=ptt[:, 0, :], in_=kt, identity=ident), reads=[("ktmp", nch), ("ident",)], writes=[("pstp", 1)])
                    P.op("dve", lambda e, ptt=ptt, g=g, nch=nch: e.tensor_copy(out=kcT[:, g, nch * 128:(nch + 1) * 128], in_=ptt[:, 0, :]),
                         reads=[("pstp", 1), ("kcT",)], writes=[("kcT",)])
                else:
                    P.op("act", lambda e, pz=pz, g=g, nch=nch: e.copy(out=vcaug[:, nch, g, 0:128], in_=pz), reads=[("psg", nch), ("vcaug",)], writes=[("vcaug",)])


SWA_W = ["w_in", "w_out", "mix_g", "qg", "kg", "sinks"]
NSA_W = ["w_in", "w_out", "mix_g", "qg", "kg", "pe_k", "w1_k", "w2_k", "pe_v", "w1_v", "w2_v"]
W_SHAPES = {
    "ffn": {"w_in": [2048, 11264], "w_out": [5632, 2048], "g": [2048]},
    "swa": {"w_in": [2048, 2560], "w_out": [2048, 2048], "mix_g": [2048], "qg": [64], "kg": [64], "sinks": [32]},
    "nsa": {"w_in": [2048, 5168], "w_out": [2048, 2048], "mix_g": [2048], "qg": [128], "kg": [128],
            "pe_k": [32, 128], "w1_k": [32, 128, 256], "w2_k": [256, 128], "pe_v": [32, 128], "w1_v": [32, 128, 256], "w2_v": [256, 128]},
}
CONST_SHAPES = {
    "pos_T": ([128, 32], I32), "pos_cmp": ([128, 2], I32), "invf8": ([128, 8], F32), "invf16": ([128, 16], F32),
    "wmask1": ([16, 128, 2, 512], BF16), "wmask4": ([16, 128, 5, 512], BF16), "cmask": ([16, 128, 2, 512], BF16),
    "tril": ([128, 512], BF16), "selA": ([16, 128, 64], F32), "selC": ([16, 128, 64], F32), "selV": ([16, 128, 64], F32),
    "overlap": ([128, 2, 64], BF16), "E": ([64, 32, 128], BF16), "ident": ([128, 128], BF16),
}


GROUPS = [[0, 1], [2, 3], [4, 5], [6, 7]]
XCH_ROWS = 256


def build_program(phases):
    import contextlib
    nc = bass.Bass("TRN2", target_bir_lowering=False)

    def inp(name, shape, dt=F32):
        return nc.dram_tensor(name, list(shape), dt, kind="ExternalInput").ap()

    def scratch(name, shape, dt):
        return nc.dram_tensor(name, list(shape), dt, kind="Internal").ap()

    kinds = set(phases)
    C = {}
    need = ["ident"]
    if "swa" in kinds:
        need += ["pos_T", "invf8", "wmask1", "tril"]
    if "nsa" in kinds:
        need += ["pos_T", "pos_cmp", "invf16", "wmask4", "cmask", "tril", "selA", "selC", "selV", "overlap", "E"]
    for n in dict.fromkeys(need):
        C[n] = inp(n, *CONST_SHAPES[n])
    h_in = inp("h_in", [2048, 2048])
    h_out = nc.dram_tensor("h", [2048, 2048], F32, kind="ExternalOutput").ap()
    h = scratch("hw", [2048, 2048], F32)
    nx = 2048 // XCH_ROWS
    hg = scratch("hg", [nx, 2 * XCH_ROWS, 2048], F32)
    W = []
    for p, k in enumerate(phases):
        W.append({n: inp("p%d_%s" % (p, n), shp) for n, shp in W_SHAPES[k].items()})
    S = {}
    if "swa" in kinds:
        S["swa"] = {"Ks": scratch("swa_Ks", [4096, 512], BF16), "Vs": scratch("swa_Vs", [4096, 256], BF16),
                    "Qs": scratch("swa_Qs", [2048, 4096], BF16), "Os": scratch("swa_Os", [2048, 2048], BF16)}
    if "nsa" in kinds:
        S["nsa"] = {n: scratch("nsa_" + n, [4096, 512], BF16) for n in ("kc", "vc", "ks", "vs", "kw", "vw")}
        S["nsa"].update(Qs=scratch("nsa_Qs", [2048, 2048], BF16), Gs=scratch("nsa_Gs", [2048, 48], F32), Os=scratch("nsa_Os", [2048, 2048], BF16))

    def kv_src(r0):
        if r0 >= 2048:
            return h[r0 - 2048:r0 - 2048 + 128, :]
        return hg[r0 // XCH_ROWS, (r0 % XCH_ROWS):(r0 % XCH_ROWS) + 128, :]

    with contextlib.ExitStack() as st:
        A, ps, ring, ar = setup_common(nc, st)
        P = Prog(nc)
        init_consts(P, A, C["ident"])
        for r in range(4):
            P.dma("sp", h[r * 512:(r + 1) * 512, :], h_in[r * 512:(r + 1) * 512, :], writes=[("cp", r)], lane=("cp", r))
        rs = {"n": 0}
        for p, k in enumerate(phases):
            P.barrier()
            nxt = phases[p + 1] if p + 1 < len(phases) else None

            def xch(js):
                for j in js:
                    src = h[j * XCH_ROWS:(j + 1) * XCH_ROWS, :]
                    dst = hg[j]
                    rows = [("hdram", r0, c) for r0 in range(j * XCH_ROWS, (j + 1) * XCH_ROWS, 128) for c in range(4)]
                    P.coll(lambda e, src=src, dst=dst: e.collective_compute("AllGather", ALU.bypass, replica_groups=GROUPS, ins=[src.opt()], outs=[dst.opt()]),
                           reads=rows, writes=[("hg", j)], lane=("xch", j))

            if k == "ffn":
                load_gain(P, A["nbuf"]["gain"], W[p]["g"])
                hook = None
                if nxt == "nsa":
                    hook = lambda tp: xch(range(0, nx // 2)) if tp == 1 else None
                ffn_phase(P, A, ps, h, W[p]["w_in"], W[p]["w_out"], ring, rs, mid_hook=hook)
                if nxt == "nsa":
                    xch(range(nx // 2, nx))
                elif nxt == "swa":
                    xch([nx - 1])
                continue
            cfg = CFG_SWA if k == "swa" else CFG_NSA
            io = dict(W[p])
            io.update(S[k])
            io.update(h_own=h, h_kv=None, kv_src=kv_src, pos_T=C["pos_T"], tril=C["tril"])
            if k == "swa":
                io.update(invf=C["invf8"], wmask=C["wmask1"])
            else:
                io.update(invf=C["invf16"], wmask=C["wmask4"], pos_cmp=C["pos_cmp"], cmask=C["cmask"], selA=C["selA"], selC=C["selC"],
                          selV=C["selV"], overlap=C["overlap"], E=C["E"])
            mixer_p1(P, A, ps, ring, rs, cfg, io)
            P.barrier()
            M = mixer_p3(P, A, ps, ring, rs, cfg, io)
            if k == "nsa":
                mixer_p2(P, A, ps, ring, rs, cfg, io, M)
                P.barrier()
            mixer_p3_run(P, A, ps, ring, rs, cfg, io, M)
            P.barrier()
            mixer_p4(P, A, ps, ring, rs, cfg, io)
        P.barrier()
        fin = []
        for r in range(4):
            fin.append(P.dma("sp", h_out[r * 512:(r + 1) * 512, :], h[r * 512:(r + 1) * 512, :], writes=[("out", r)], lane=("cpo", r)))
        P.emit(final_waits=fin)
    return nc


_PROGS = {}


def _core_consts(positions, c):
    b, half = c // 2, c % 2
    pos = np.asarray(positions[b]).astype(np.int32)
    if half == 1:
        pos_l = pos
    else:
        pos_l = np.concatenate([np.zeros(2048, np.int32), pos[:2048]])
    nu = np.minimum(np.arange(256), 254)
    out = {"pos_T": np.ascontiguousarray(pos_l.reshape(32, 128).T),
           "pos_cmp": np.ascontiguousarray(pos_l[16 * nu + 31].reshape(2, 128).T),
           "invf8": invf_table(16), "invf16": invf_table(32)}
    h1 = host_consts(half, 1)
    h4 = host_consts(half, 4)
    out["wmask1"] = h1["wmask"]
    out["wmask4"] = h4["wmask"]
    out["cmask"] = h4["cmask"]
    for n in ("tril", "selA", "selC", "selV", "overlap", "E", "ident"):
        out[n] = h4[n]
    return out


def _layer_weights(inp, i, kind, which=None):
    j = i // 2
    if kind == "ffn" and which == 1:
        return {"w_in": inp["ffn1_w_in"][i], "w_out": inp["ffn1_w_out"][i], "g": inp["ffn1_norm"][i]}
    if kind == "ffn":
        return {"w_in": inp["ffn2_w_in"][i], "w_out": inp["ffn2_w_out"][i], "g": inp["ffn2_norm"][i]}
    if kind == "swa":
        return {"w_in": inp["swa_w_in"][j], "w_out": inp["swa_w_out"][j], "mix_g": inp["mix_norm"][i], "qg": inp["swa_q_norm"][j],
                "kg": inp["swa_k_norm"][j], "sinks": inp["swa_sinks"][j]}
    return {"w_in": inp["nsa_w_in"][j], "w_out": inp["nsa_w_out"][j], "mix_g": inp["mix_norm"][i], "qg": inp["nsa_q_norm"][j], "kg": inp["nsa_k_norm"][j],
            "pe_k": inp["nsa_cmp_pe_k"][j], "w1_k": inp["nsa_cmp_w1_k"][j], "w2_k": inp["nsa_cmp_w2_k"][j],
            "pe_v": inp["nsa_cmp_pe_v"][j], "w1_v": inp["nsa_cmp_w1_v"][j], "w2_v": inp["nsa_cmp_w2_v"][j]}


def kernel(**inputs):
    inp = {k: np.asarray(v) for k, v in inputs.items()}
    x = inp["x"].astype(np.float32, copy=False)
    positions = inp["positions"]
    depth = 4
    consts = [_core_consts(positions, c) for c in range(8)]
    grp = []
    for i in range(depth):
        grp += [("ffn", i, 1), ("swa" if i % 2 == 0 else "nsa", i, None), ("ffn", i, 2)]
    phases = tuple(g[0] for g in grp)
    if phases not in _PROGS:
        _PROGS[phases] = build_program(phases)
    nc = _PROGS[phases]
    wmaps = {}
    for p, (kind, i, which) in enumerate(grp):
        for n, a in _layer_weights(inp, i, kind, which).items():
            wmaps["p%d_%s" % (p, n)] = np.ascontiguousarray(a, dtype=np.float32)
    need = ["ident", "pos_T", "invf8", "wmask1", "tril", "pos_cmp", "invf16", "wmask4", "cmask", "selA", "selC", "selV", "overlap", "E"]
    in_maps = []
    for c in range(8):
        m = dict(wmaps)
        m["h_in"] = np.ascontiguousarray(x[c // 2, (c % 2) * 2048:(c % 2 + 1) * 2048])
        for n in need:
            m[n] = consts[c][n]
        in_maps.append(m)
    res = run_bass_kernel_spmd(nc, in_maps, core_ids=list(range(8)))
    out = np.empty((4, 4096, 2048), np.float32)
    for c in range(8):
        out[c // 2, (c % 2) * 2048:(c % 2 + 1) * 2048] = np.asarray(res.results[c]["h"])
    return out
```
